# Optimizing a Trainium2 kernel written in Bass

```python
import math
import jax, jax.numpy as jnp
from jax import lax
import numpy as np

D_MODEL = 1024
BATCH = 16
SEQ = 256
DEPTH = 4
DEC_BATCH = 8
DEC_SEQ = 4096
PAST_LEN = 512

GRID_W = 64
N_MIXERS = 3
N_LAYERS_A = (DEPTH + 2) // 3
N_LAYERS_B = (DEPTH + 1) // 3
N_LAYERS_C = DEPTH // 3
D_FF = 4 * D_MODEL
N_MOD = 6
RMS_EPS = 1e-6
ROPE_BASE = 10000.0
NEG_INF = -1e30

ATT_HEADS = 16
ATT_KV_HEADS = 4
ATT_HEAD_DIM = 64
ATT_GROUPS = ATT_HEADS // ATT_KV_HEADS
ATT_QKV_W = (ATT_HEADS + 2 * ATT_KV_HEADS) * ATT_HEAD_DIM
WINDOW = 128
BLOCK = 128

RET_HEADS = 4
RET_DK = 256
RET_DV = 512
RET_CHUNK = 128
RET_IN_W = 2 * RET_HEADS * RET_DK + 2 * RET_HEADS * RET_DV

D_RNN = 1024
LRU_BLOCKS = 8
LRU_BW = D_RNN // LRU_BLOCKS
CONV_W = 4
LRU_C = 8.0

kernel_name = "hybrid_flow_trunk_step"


def rms_norm(x, g):
    xf = x.astype(jnp.float32)
    y = xf * lax.rsqrt(jnp.mean(xf * xf, axis=-1, keepdims=True) + RMS_EPS)
    return (y * g.astype(jnp.float32)).astype(x.dtype)


def modulation(cvec, w_ada, b_ada):
    m = jax.nn.silu(cvec) @ w_ada + b_ada
    return jnp.split(m[:, None, :], N_MOD, axis=-1)


def modulate(x, g, shift, scale):
    return rms_norm(x, g) * (1.0 + scale) + shift


def grid_positions(T):
    rows = T // GRID_W
    row = jnp.repeat(jnp.arange(rows, dtype=jnp.int32), GRID_W)
    col = jnp.tile(jnp.arange(GRID_W, dtype=jnp.int32), rows)
    return row, col


def rope_1d(x, pos):
    half = x.shape[-1] // 2
    inv = ROPE_BASE ** (-jnp.arange(half, dtype=jnp.float32) / half)
    ang = pos.astype(jnp.float32)[:, None] * inv[None, :]
    cos = jnp.cos(ang)[:, None, :]
    sin = jnp.sin(ang)[:, None, :]
    xf = x.astype(jnp.float32)
    x1, x2 = xf[..., :half], xf[..., half:]
    return jnp.concatenate([x1 * cos - x2 * sin, x2 * cos + x1 * sin], axis=-1).astype(x.dtype)


def axial_rope(x):
    row, col = grid_positions(x.shape[1])
    h = x.shape[-1] // 2
    return jnp.concatenate([rope_1d(x[..., :h], row), rope_1d(x[..., h:], col)], axis=-1)


def attn_project(h, w_in, q_gain, k_gain):
    B, T, _ = h.shape
    qkv = h @ w_in
    q_end = ATT_HEADS * ATT_HEAD_DIM
    k_end = q_end + ATT_KV_HEADS * ATT_HEAD_DIM
    q = rms_norm(qkv[..., :q_end].reshape(B, T, ATT_HEADS, ATT_HEAD_DIM), q_gain)
    k = rms_norm(qkv[..., q_end:k_end].reshape(B, T, ATT_KV_HEADS, ATT_HEAD_DIM), k_gain)
    v = qkv[..., k_end:].reshape(B, T, ATT_KV_HEADS, ATT_HEAD_DIM)
    return q, k, v


def sink_column(sink, B, Q):
    s = sink.astype(jnp.float32).reshape(ATT_KV_HEADS, ATT_GROUPS)[None, :, :, None, None]
    return jnp.broadcast_to(s, (B, ATT_KV_HEADS, ATT_GROUPS, Q, 1))


def context_attention(q, k, v, sink):
    B, L, H, d = q.shape
    qg = q.reshape(B, L, ATT_KV_HEADS, ATT_GROUPS, d)
    s = jnp.einsum('bqhgd,bkhd->bhgqk', qg, k, preferred_element_type=jnp.float32) * (d ** -0.5)
    p = jax.nn.softmax(jnp.concatenate([s, sink_column(sink, B, L)], axis=-1), axis=-1)[..., :L]
    o = jnp.einsum('bhgqk,bkhd->bqhgd', p.astype(v.dtype), v)
    return o.reshape(B, L, H, d)


def latent_window_attention(q, k, v, k_ctx, v_ctx, sink):
    B, T, H, d = q.shape
    L = k_ctx.shape[1]
    nb = T // BLOCK
    span = BLOCK + 2 * WINDOW
    scale = d ** -0.5
    qg = q.reshape(B, T, ATT_KV_HEADS, ATT_GROUPS, d)
    pad = ((0, 0), (WINDOW, WINDOW), (0, 0), (0, 0))
    kp = jnp.pad(k, pad)
    vp = jnp.pad(v, pad)
    rel = jnp.arange(span)[None, :] - WINDOW - jnp.arange(BLOCK)[:, None]
    near = jnp.abs(rel) <= WINDOW
    sink_col = sink_column(sink, B, BLOCK)

    def one_block(n):
        start = n * BLOCK
        qb = lax.dynamic_slice_in_dim(qg, start, BLOCK, axis=1)
        kb = lax.dynamic_slice_in_dim(kp, start, span, axis=1)
        vb = lax.dynamic_slice_in_dim(vp, start, span, axis=1)
        kpos = start - WINDOW + jnp.arange(span)
        valid = near & ((kpos >= 0) & (kpos < T))[None, :]
        s_lat = jnp.einsum('bqhgd,bkhd->bhgqk', qb, kb, preferred_element_type=jnp.float32) * scale
        s_lat = jnp.where(valid, s_lat, NEG_INF)
        s_ctx = jnp.einsum('bqhgd,bkhd->bhgqk', qb, k_ctx, preferred_element_type=jnp.float32) * scale
        p = jax.nn.softmax(jnp.concatenate([s_lat, s_ctx, sink_col], axis=-1), axis=-1).astype(v.dtype)
        o = (jnp.einsum('bhgqk,bkhd->bqhgd', p[..., :span], vb)
             + jnp.einsum('bhgqk,bkhd->bqhgd', p[..., span:span + L], v_ctx))
        return o.reshape(B, BLOCK, H, d)

    out = lax.map(one_block, jnp.arange(nb))
    return jnp.moveaxis(out, 0, 1).reshape(B, T, H, d)


def retention_scan(q, k, v, log_gamma, s0, inclusive):
    B, T, H, dk = q.shape
    dv = v.shape[-1]
    C = RET_CHUNK
    nc = T // C
    qc = q.reshape(B, nc, C, H, dk)
    kc = k.reshape(B, nc, C, H, dk)
    vc = v.reshape(B, nc, C, H, dv)
    lg = log_gamma.astype(jnp.float32)
    idx = jnp.arange(C, dtype=jnp.float32)
    diff = idx[:, None] - idx[None, :]
    mask = (diff >= 0) if inclusive else (diff > 0)
    decay = jnp.where(mask[None], jnp.exp(lg[:, None, None] * jnp.maximum(diff, 0.0)[None]), 0.0)
    scores = jnp.einsum('bnihd,bnjhd->bnhij', qc, kc, preferred_element_type=jnp.float32) * decay
    o_intra = jnp.einsum('bnhij,bnjhe->bnihe', scores, vc)
    w_state = jnp.exp(lg[:, None] * (C - 1.0 - idx)[None, :])
    w_query = jnp.exp(lg[:, None] * (idx + 1.0)[None, :])
    gamma_chunk = jnp.exp(lg * C)[None, :, None, None]

    def step(s, blk):
        qn, kn, vn = blk
        o_cross = jnp.einsum('bihd,hi,bhde->bihe', qn, w_query, s)
        kv = jnp.einsum('bjhd,hj,bjhe->bhde', kn, w_state, vn)
        return gamma_chunk * s + kv, o_cross

    s_fin, o_cross = lax.scan(step, s0.astype(jnp.float32),
                              (jnp.moveaxis(qc, 1, 0), jnp.moveaxis(kc, 1, 0), jnp.moveaxis(vc, 1, 0)))
    o = o_intra + jnp.moveaxis(o_cross, 0, 1)
    return o.reshape(B, T, H, dv), s_fin


def retention_mixer(h, w_in, w_out, gn_gain, log_decay, s0_f, s0_b, rotate):
    B, T, _ = h.shape
    proj = h @ w_in
    e1 = RET_HEADS * RET_DK
    e2 = 2 * e1
    e3 = e2 + RET_HEADS * RET_DV
    q = proj[..., :e1].reshape(B, T, RET_HEADS, RET_DK)
    k = proj[..., e1:e2].reshape(B, T, RET_HEADS, RET_DK)
    v = proj[..., e2:e3].reshape(B, T, RET_HEADS, RET_DV)
    g = proj[..., e3:]
    if rotate:
        q, k = axial_rope(q), axial_rope(k)
    k = k * (RET_DK ** -0.5)
    o_f, s_f = retention_scan(q, k, v, log_decay[0], s0_f, True)
    o_b, s_b = retention_scan(jnp.flip(q, 1), jnp.flip(k, 1), jnp.flip(v, 1), log_decay[1], s0_b, False)
    o = o_f + jnp.flip(o_b, 1)
    mu = jnp.mean(o, axis=-1, keepdims=True)
    var = jnp.mean(jnp.square(o - mu), axis=-1, keepdims=True)
    o = ((o - mu) * lax.rsqrt(var + RMS_EPS)).reshape(B, T, RET_HEADS * RET_DV) * gn_gain.astype(jnp.float32)
    y = (jax.nn.silu(g) * o.astype(h.dtype)) @ w_out
    return y, s_f, s_b


def centred_depthwise_conv(x, w, b):
    left = CONV_W // 2
    right = CONV_W - 1 - left
    y = lax.conv_general_dilated(x, w[:, None, :], window_strides=(1,), padding=[(left, right)],
                                 dimension_numbers=('NWC', 'WIO', 'NWC'), feature_group_count=x.shape[-1])
    return y + b


def _linear_combine(left, right):
    a_l, u_l = left
    a_r, u_r = right
    return a_l * a_r, a_r * u_l + u_r


def rglru_scan(xc, w_r, b_r, w_i, b_i, lam, h0):
    B, T, _ = xc.shape
    xf = xc.astype(jnp.float32)
    xb = xf.reshape(B, T, LRU_BLOCKS, LRU_BW)
    r = jax.nn.sigmoid(jnp.einsum('btnc,ncd->btnd', xb, w_r.astype(jnp.float32)).reshape(B, T, D_RNN) + b_r)
    i = jax.nn.sigmoid(jnp.einsum('btnc,ncd->btnd', xb, w_i.astype(jnp.float32)).reshape(B, T, D_RNN) + b_i)
    log_a = -LRU_C * r * jax.nn.softplus(-lam.astype(jnp.float32))
    a = jnp.exp(log_a)
    u = jnp.sqrt(-jnp.expm1(2.0 * log_a)) * (i * xf)
    u = u.at[:, 0].add(a[:, 0] * h0.astype(jnp.float32))
    _, hs = lax.associative_scan(_linear_combine, (a, u), axis=1)
    return hs, hs[:, -1]


def lru_mixer(h, w_in, conv_w, conv_b, w_r, b_r, w_i, b_i, lam, w_out, h0_f, h0_b):
    proj = h @ w_in
    gate, xr = proj[..., :D_RNN], proj[..., D_RNN:]
    xc = centred_depthwise_conv(xr, conv_w, conv_b)
    h_f, s_f = rglru_scan(xc, w_r[0], b_r[0], w_i[0], b_i[0], lam[0], h0_f)
    h_b, s_b = rglru_scan(jnp.flip(xc, 1), w_r[1], b_r[1], w_i[1], b_i[1], lam[1], h0_b)
    rec = (h_f + jnp.flip(h_b, 1)).astype(h.dtype)
    y = (jax.nn.gelu(gate) * rec) @ w_out
    return y, s_f, s_b


def sq_relu_mlp(h, w_up, w_down):
    return jnp.square(jax.nn.relu(h @ w_up)) @ w_down


def setup_inputs(seed: int = 0) -> dict:
    key = jax.random.key(seed)
    ks = iter(jax.random.split(key, 48))
    D = D_MODEL

    def nrm(shape, scale):
        return jax.random.normal(next(ks), shape, jnp.float32) * scale

    ret_base = jnp.log1p(-(2.0 ** (-5.0 - jnp.arange(RET_HEADS, dtype=jnp.float32))))
    u = jax.random.uniform(next(ks), (N_LAYERS_C, 2, D_RNN), jnp.float32, 0.9, 0.999)
    return {
        'x_prompt': nrm((BATCH, SEQ, D), 1.0),
        'x_sample': nrm((DEC_BATCH, DEC_SEQ, D), 1.0),
        'cache_attn_k': nrm((DEC_BATCH, N_LAYERS_A, PAST_LEN, ATT_KV_HEADS, ATT_HEAD_DIM), 1.0),
        'cache_attn_v': nrm((DEC_BATCH, N_LAYERS_A, PAST_LEN, ATT_KV_HEADS, ATT_HEAD_DIM), 1.0),
        'state_ret': nrm((DEC_BATCH, N_LAYERS_B, 2, RET_HEADS, RET_DK, RET_DV), 0.5),
        'state_lru': nrm((DEC_BATCH, N_LAYERS_C, 2, D_RNN), 0.5),
        'c': nrm((DEC_BATCH, D), 1.0),
        'c_ctx': nrm((D,), 1.0),
        'norm_mix': 1.0 + nrm((DEPTH, D), 0.02),
        'norm_mlp': 1.0 + nrm((DEPTH, D), 0.02),
        'w_ada': nrm((DEPTH, D, N_MOD * D), 0.5 * D ** -0.5),
        'b_ada': nrm((DEPTH, N_MOD * D), 0.02),
        'w_up': nrm((DEPTH, D, D_FF), D ** -0.5),
        'w_down': nrm((DEPTH, D_FF, D), D_FF ** -0.5),
        'attn_w_in': nrm((N_LAYERS_A, D, ATT_QKV_W), D ** -0.5),
        'attn_w_out': nrm((N_LAYERS_A, ATT_HEADS * ATT_HEAD_DIM, D), (ATT_HEADS * ATT_HEAD_DIM) ** -0.5),
        'attn_q_gain': 1.0 + nrm((N_LAYERS_A, ATT_HEAD_DIM), 0.02),
        'attn_k_gain': 1.0 + nrm((N_LAYERS_A, ATT_HEAD_DIM), 0.02),
        'attn_sink': nrm((N_LAYERS_A, ATT_HEADS), 0.5),
        'ret_w_in': nrm((N_LAYERS_B, D, RET_IN_W), D ** -0.5),
        'ret_w_out': nrm((N_LAYERS_B, RET_HEADS * RET_DV, D), (RET_HEADS * RET_DV) ** -0.5),
        'ret_gn_gain': 1.0 + nrm((N_LAYERS_B, RET_HEADS * RET_DV), 0.02),
        'ret_log_decay': ret_base[None, None, :] * jnp.exp(nrm((N_LAYERS_B, 2, RET_HEADS), 0.05)),
        'lru_w_in': nrm((N_LAYERS_C, D, 2 * D_RNN), D ** -0.5),
        'lru_conv_w': nrm((N_LAYERS_C, CONV_W, D_RNN), CONV_W ** -0.5),
        'lru_conv_b': nrm((N_LAYERS_C, D_RNN), 0.02),
        'lru_w_r': nrm((N_LAYERS_C, 2, LRU_BLOCKS, LRU_BW, LRU_BW), LRU_BW ** -0.5),
        'lru_b_r': nrm((N_LAYERS_C, 2, D_RNN), 0.02),
        'lru_w_i': nrm((N_LAYERS_C, 2, LRU_BLOCKS, LRU_BW, LRU_BW), LRU_BW ** -0.5),
        'lru_b_i': nrm((N_LAYERS_C, 2, D_RNN), 0.02),
        'lru_lambda': jnp.log(u) - jnp.log1p(-u),
        'lru_w_out': nrm((N_LAYERS_C, D_RNN, D), D_RNN ** -0.5),
    }


def reference(x_prompt, x_sample, cache_attn_k, cache_attn_v, state_ret, state_lru, c, c_ctx,
              norm_mix, norm_mlp, w_ada, b_ada, w_up, w_down,
              attn_w_in, attn_w_out, attn_q_gain, attn_k_gain, attn_sink,
              ret_w_in, ret_w_out, ret_gn_gain, ret_log_decay,
              lru_w_in, lru_conv_w, lru_conv_b, lru_w_r, lru_b_r, lru_w_i, lru_b_i, lru_lambda, lru_w_out):
    Bp, Lp, _ = x_prompt.shape
    Bs, Ts, _ = x_sample.shape
    c_prompt = jnp.broadcast_to(c_ctx[None, :], (Bp, D_MODEL))
    xp, xs = x_prompt, x_sample
    new_k, new_v, new_ret, new_lru = [], [], [], []
    for layer in range(DEPTH):
        kind = layer % N_MIXERS
        slot = layer // N_MIXERS
        sh_ap, sc_ap, g_ap, sh_mp, sc_mp, g_mp = modulation(c_prompt, w_ada[layer], b_ada[layer])
        sh_as, sc_as, g_as, sh_ms, sc_ms, g_ms = modulation(c, w_ada[layer], b_ada[layer])
        hp = modulate(xp, norm_mix[layer], sh_ap, sc_ap)
        hs = modulate(xs, norm_mix[layer], sh_as, sc_as)
        if kind == 0:
            q, k, v = attn_project(hp, attn_w_in[slot], attn_q_gain[slot], attn_k_gain[slot])
            yp = context_attention(q, k, v, attn_sink[slot]).reshape(Bp, Lp, -1) @ attn_w_out[slot]
            new_k.append(k)
            new_v.append(v)
            q, k, v = attn_project(hs, attn_w_in[slot], attn_q_gain[slot], attn_k_gain[slot])
            q, k = axial_rope(q), axial_rope(k)
            ys = latent_window_attention(q, k, v, cache_attn_k[:, slot], cache_attn_v[:, slot],
                                         attn_sink[slot]).reshape(Bs, Ts, -1) @ attn_w_out[slot]
        elif kind == 1:
            zero = jnp.zeros((Bp, RET_HEADS, RET_DK, RET_DV), jnp.float32)
            yp, s_f, s_b = retention_mixer(hp, ret_w_in[slot], ret_w_out[slot], ret_gn_gain[slot],
                                           ret_log_decay[slot], zero, zero, False)
            new_ret.append(jnp.stack([s_f, s_b], axis=1).astype(x_prompt.dtype))
            ys, _, _ = retention_mixer(hs, ret_w_in[slot], ret_w_out[slot], ret_gn_gain[slot],
                                       ret_log_decay[slot], state_ret[:, slot, 0], state_ret[:, slot, 1], True)
        else:
            zero = jnp.zeros((Bp, D_RNN), jnp.float32)
            yp, s_f, s_b = lru_mixer(hp, lru_w_in[slot], lru_conv_w[slot], lru_conv_b[slot], lru_w_r[slot],
                                     lru_b_r[slot], lru_w_i[slot], lru_b_i[slot], lru_lambda[slot],
                                     lru_w_out[slot], zero, zero)
            new_lru.append(jnp.stack([s_f, s_b], axis=1).astype(x_prompt.dtype))
            ys, _, _ = lru_mixer(hs, lru_w_in[slot], lru_conv_w[slot], lru_conv_b[slot], lru_w_r[slot],
                                 lru_b_r[slot], lru_w_i[slot], lru_b_i[slot], lru_lambda[slot],
                                 lru_w_out[slot], state_lru[:, slot, 0], state_lru[:, slot, 1])
        xp = xp + g_ap * yp
        xs = xs + g_as * ys
        xp = xp + g_mp * sq_relu_mlp(modulate(xp, norm_mlp[layer], sh_mp, sc_mp), w_up[layer], w_down[layer])
        xs = xs + g_ms * sq_relu_mlp(modulate(xs, norm_mlp[layer], sh_ms, sc_ms), w_up[layer], w_down[layer])
    new_attn_k = jnp.stack(new_k, axis=1)
    new_attn_v = jnp.stack(new_v, axis=1)
    new_state_ret = jnp.stack(new_ret, axis=1)
    new_state_lru = jnp.stack(new_lru, axis=1)
    return (xp, xs, new_attn_k, new_attn_v, new_state_ret, new_state_lru)
```

```python
import contextlib
import os
import numpy as np
import ml_dtypes
import concourse.bass as bass
import concourse.mybir as mybir
from concourse.bass_utils import run_bass_kernel_spmd

F32 = mybir.dt.float32
BF16 = mybir.dt.bfloat16
AF = mybir.ActivationFunctionType
ALU = mybir.AluOpType

ENGS = ['pe', 'act', 'dve', 'pool', 'sp']
N_DMA_SEMS = 40
SAME_ENG_SYNC = True
SB_BASE = 18560
SB_TOP = 229376

NTOK = 4608
NS = 4096
EPS = 1e-6


class Sched:
    def __init__(self, nc):
        self.nc = nc
        self.ops = {e: [] for e in ENGS}
        self.cnt = {e: 0 for e in ENGS}
        self.known = {e: {} for e in ENGS}
        self.buf = {}
        self.dma_uses = [0] * N_DMA_SEMS
        self.dma_rr = 0
        self.sb_off = SB_BASE
        self.sb_hi = SB_BASE
        self.names = 0
        self.phase = 'init'
        self.scopes = False

    def sb(self, shape, dtype, name=None):
        esz = 2 if dtype == BF16 else 4
        n = 1
        for s in shape[1:]:
            n *= s
        nbytes = (n * esz + 63) // 64 * 64
        self.names += 1
        nm = f"{name or 't'}_{self.names}"
        t = self.nc.alloc_sbuf_tensor_at(nm, list(shape), dtype, offset=self.sb_off)
        self.sb_off += nbytes
        self.sb_hi = max(self.sb_hi, self.sb_off)
        assert self.sb_off <= SB_TOP, f"SBUF overflow {self.sb_off} ({nm})"
        return t

    def sb_mark(self):
        return self.sb_off

    def sb_reset(self, mark):
        self.barrier()
        self.sb_off = mark

    def _deps(self, eng, reads, writes):
        deps = {}

        def add(tok):
            if tok is None:
                return
            k, v = tok
            if deps.get(k, 0) < v:
                deps[k] = v
        for r in reads:
            b = self.buf.get(r)
            if b:
                add(b['w'])
        for w in writes:
            b = self.buf.get(w)
            if b:
                add(b['w'])
                for k, v in b['r'].items():
                    add((k, v))
        out = []
        kn = self.known[eng]
        for k, v in deps.items():
            if k == eng and (eng == 'pe' or not SAME_ENG_SYNC):
                continue
            if kn.get(k, 0) >= v:
                continue
            kn[k] = v
            out.append((k, v))
        return out

    def _mark(self, tok, reads, writes):
        k, v = tok
        for r in reads:
            b = self.buf.setdefault(r, {'w': None, 'r': {}})
            if b['r'].get(k, 0) < v:
                b['r'][k] = v
        for w in writes:
            self.buf[w] = {'w': tok, 'r': {}}

    def op(self, eng, fn, reads=(), writes=()):
        waits = self._deps(eng, reads, writes)
        self.cnt[eng] += 1
        tok = (eng, self.cnt[eng])
        self._mark(tok, reads, writes)
        self.ops[eng].append((fn, waits, True, self.phase))

    def dma(self, fn, reads=(), writes=(), q='sp'):
        s = self.dma_rr
        self.dma_rr = (self.dma_rr + 1) % N_DMA_SEMS
        waits = self._deps(q, reads, writes)
        prev = self.dma_uses[s]
        key = ('dma', s)
        if prev > 0 and self.known[q].get(key, 0) < 16 * prev:
            self.known[q][key] = 16 * prev
            waits.append((key, 16 * prev))
        self.dma_uses[s] += 1
        tok = (key, 16 * self.dma_uses[s])
        self._mark(tok, reads, writes)
        self.ops[q].append((fn, waits, key, self.phase))

    def barrier(self):
        allk = {e: self.cnt[e] for e in ENGS if self.cnt[e] > 0}
        for s in range(N_DMA_SEMS):
            if self.dma_uses[s]:
                allk[('dma', s)] = 16 * self.dma_uses[s]
        for e in ENGS:
            waits = []
            for k, v in allk.items():
                if k == e and e in ('pe', 'sp'):
                    continue
                if self.known[e].get(k, 0) >= v:
                    continue
                self.known[e][k] = v
                waits.append((k, v))
            if waits:
                self.ops[e].append((None, waits, False, self.phase))

    def emit(self):
        nc = self.nc
        self.barrier()
        with contextlib.ExitStack() as st:
            sems = {}
            for e in ENGS:
                sems[e] = st.enter_context(nc.semaphore(f"s_{e}"))
            for s in range(N_DMA_SEMS):
                sems[('dma', s)] = st.enter_context(nc.semaphore(f"s_dma{s}"))
            block = st.enter_context(nc.Block())

            def run(ename, eng):
                cur = [None, None]

                def setscope(lbl):
                    if not self.scopes or lbl == cur[0]:
                        return
                    if cur[1] is not None:
                        cur[1].__exit__(None, None, None)
                    cur[0] = lbl
                    cur[1] = nc.named_scope(lbl) if lbl is not None else None
                    if cur[1] is not None:
                        cur[1].__enter__()
                for fn, waits, inc, ph in self.ops[ename]:
                    setscope(ph)
                    for k, v in waits:
                        eng.wait_ge(sems[k], v)
                    if fn is None:
                        continue
                    ins = fn(eng)
                    if inc is True:
                        ins.then_inc(sems[ename], 1)
                    elif inc is not False:
                        ins.then_inc(sems[inc], 16)
                setscope(None)

            @block.tensor
            def _(e):
                run('pe', e)

            @block.scalar
            def _(e):
                run('act', e)

            @block.vector
            def _(e):
                run('dve', e)

            @block.gpsimd
            def _(e):
                run('pool', e)

            @block.sync
            def _(e):
                run('sp', e)


def fsize(t):
    n = 1
    for s in t.shape[1:]:
        n *= s
    return n


def V(t, off, dims, p0=0, npart=128):
    F = fsize(t)
    return bass.AP(t, p0 * F + off, [[F, npart]] + [list(d) for d in dims])


def build_nc(depth_run=4, scopes=False):
    nc = bass.Bass("TRN2", target_bir_lowering=False)
    S = Sched(nc)
    S.scopes = scopes
    D = {}

    def din(name, shape, dt=F32):
        D[name] = nc.dram_tensor(name, list(shape), dt, kind="ExternalInput").ap()

    def dout(name, shape, dt=F32):
        D[name] = nc.dram_tensor(name, list(shape), dt, kind="ExternalOutput").ap()

    def dscr(name, shape, dt=F32):
        D[name] = nc.dram_tensor(name, list(shape), dt).ap()

    din('xT', [1024, NTOK]); din('cT', [128, 8, 2])
    din('kctx', [2, 128, 4, 512]); din('vctx', [2, 128, 4, 4, 64])
    din('sret', [2, 4, 128, 2, 512]); din('slru', [128, 2, 8])
    din('w_ada', [4, 1024, 6144]); din('b_adaT', [128, 4, 48]); din('nmT', [128, 2, 4, 8])
    din('w_up', [4, 1024, 4096]); din('w_down', [4, 4096, 1024])
    din('attn_win', [2, 1024, 1792]); din('attn_wout', [2, 1024, 1024])
    din('qgT', [128, 2]); din('kgT', [128, 2]); din('sinkB', [128, 2, 16])
    din('cosA', [128, NS]); din('sinA', [128, NS]); din('rotA', [128, 128]); din('bdones', [128, 128])
    din('ident', [128, 128]); din('mlo', [128, 128]); din('mhi', [128, 128]); din('mgt', [128, 128])
    din('ret_win', [1024, 6144]); din('ret_wout', [2048, 1024]); din('gainB', [128, 2048]); din('ldB', [128, 8])
    din('cosR', [128, NS]); din('sinR', [128, NS])
    din('MfT', [128, 128]); din('MbT', [128, 128]); din('iq1', [128, 128]); din('iqb', [128, 128])
    din('cstf', [128, 1]); din('cstb', [128, 1])
    din('lru_win', [1024, 2048]); din('lru_wout', [1024, 1024]); din('convT', [128, 5, 8])
    din('lru_wr', [128, 2, 8, 128]); din('lru_wi', [128, 2, 8, 128]); din('lru_bT', [128, 3, 2, 8])
    dout('yT', [1024, NTOK]); dout('k_out', [2, 256, 512]); dout('v_out', [2, 512, 256])
    dout('ret_out', [2, 2, 4, 128, 2, 512]); dout('lru_out', [2, 2, 1024])
    dscr('xres', [1024, NTOK]); dscr('qT_d', [1024, NTOK], BF16)
    dscr('rq_d', [1024, NTOK], BF16); dscr('rk_d', [1024, NTOK], BF16)
    dscr('ktok_d', [NTOK, 1024], BF16); dscr('vtok_d', [NTOK, 2048], BF16); dscr('sg_d', [NTOK, 2048], BF16)
    dscr('uT_d', [2048, NTOK], BF16)
    dscr('xr_d', [1024, NTOK]); dscr('gate_d', [1024, NTOK], BF16); dscr('yin_d', [1024, NTOK], BF16)

    ps = [nc.alloc_psum_tensor(f"ps{i}", [128, 512], F32) for i in range(8)]
    psbf = ps[7][:].bitcast(BF16)

    def psbf3(k):
        return psbf[:, 0:k * 128].rearrange("p (c n) -> p c n", n=128)

    def pipeline(n, stages, desc=False):
        maxlag = max(lg for lg, _ in stages)
        if desc:
            stages = sorted(stages, key=lambda x: -x[0])
        for t in range(n + maxlag):
            for lg, fn in stages:
                i = t - lg
                if 0 <= i < n:
                    fn(i)

    def act(out, in_, func, reads, writes, bias=None, scale=None):
        kw = {}
        if bias is not None:
            kw['bias'] = bias
        if scale is not None:
            kw['scale'] = scale
        S.op('act', lambda e: e.activation(out=out, in_=in_, func=func, **kw), reads, writes)

    def mm(out, lhsT, rhs, start, stop, reads, writes):
        S.op('pe', lambda e: e.matmul(out, lhsT, rhs, start=start, stop=stop), reads, writes)

    def tt(eng, out, in0, in1, op, reads, writes):
        S.op(eng, lambda e: e.tensor_tensor(out, in0, in1, op), reads, writes)

    def ts(eng, out, in0, s1, s2, op0, op1, reads, writes):
        S.op(eng, lambda e: e.tensor_scalar(out, in0, s1, s2, op0, op1), reads, writes)

    def ts1(eng, out, in0, s1, op0, reads, writes):
        S.op(eng, lambda e: e.tensor_scalar(out, in0, s1, None, op0), reads, writes)

    def stt(eng, out, in0, sc, in1, op0, op1, reads, writes):
        S.op(eng, lambda e: e.scalar_tensor_tensor(out, in0, sc, in1, op0, op1), reads, writes)

    def dma(out, in_, reads, writes, q='sp'):
        S.dma(lambda e: e.dma_start(out=out, in_=in_), reads, writes, q=q)

    castn = [0]

    def cast_load(dst, src, key, nsplit=1):
        castn[0] += 1
        dma(dst, src, [], [key, ('castq', castn[0] % 2)], q='pool')

    def rows(ap2d):
        return ap2d.rearrange("(c p) n -> p c n", p=128)

    ones_bf = S.sb([128, 128], BF16, "ones")
    ident_bf = S.sb([128, 128], BF16, "ident")
    modT = S.sb([128, 4, 48, 2], F32, "modT")
    Gc = S.sb([128, 4, 2, 8, 2], F32, "Gc")
    nmT = S.sb([128, 2, 4, 8], F32, "nmT")
    S.op('pool', lambda e: e.memset(ones_bf[:], 1.0), [], ['ones'])
    epsb = S.sb([128, 1], F32, "epsb"); oneb = S.sb([128, 1], F32, "oneb"); lnb = S.sb([128, 1], F32, "lnb")
    S.op('pool', lambda e: e.memset(epsb[:], EPS), [], ['epsb'])
    S.op('pool', lambda e: e.memset(oneb[:], 1.0), [], ['oneb'])
    S.op('pool', lambda e: e.memset(lnb[:], -2.772588722239781), [], ['lnb'])
    cast_load(ident_bf[:], D['ident'], 'ident')
    dma(nmT[:], D['nmT'], [], ['nmT'])
    base_mark = S.sb_mark()

    def phase_mod():
        S.phase = 'mod'
        cT = S.sb([128, 8, 2], F32, "cT")
        sc = S.sb([128, 8, 2], BF16, "sc")
        bT = S.sb([128, 4, 48], F32, "bT")
        wf = [S.sb([128, 8, 1024], F32, f"waf{i}") for i in range(2)]
        wa = [S.sb([128, 8, 1024], BF16, f"wa{i}") for i in range(2)]
        dma(cT[:], D['cT'], [], ['cT'])
        dma(bT[:], D['b_adaT'], [], ['bT'])
        act(sc[:], cT[:], AF.Silu, ['cT'], ['sc'])
        it = 0
        ceng = ['act', 'dve', 'pool', 'act', 'dve', 'act', 'dve', 'pool']
        for l in range(4):
            pb = ps[l % 2]
            for piece in range(6):
                p2 = it % 2
                it += 1
                src = rows(D['w_ada'][l])[:, :, piece * 1024:(piece + 1) * 1024]
                for k in range(8):
                    dma(wf[p2][:, k, :], src[:, k, :], [], [('waf', p2, k)])
                for k in range(8):
                    if ceng[k] == 'act':
                        act(wa[p2][:, k, :], wf[p2][:, k, :], AF.Copy, [('waf', p2, k)], [('wa', p2, k)])
                    else:
                        S.op(ceng[k], lambda e, p2=p2, k=k: e.tensor_copy(wa[p2][:, k, :], wf[p2][:, k, :]),
                             [('waf', p2, k)], [('wa', p2, k)])
                for n in range(8):
                    j = piece * 8 + n
                    for k in range(8):
                        mm(pb[:, j * 2:j * 2 + 2], wa[p2][:, k, n * 128:(n + 1) * 128], sc[:, k, :],
                           k == 0, k == 7, [('wa', p2, k), 'sc'], [('ps', l % 2)])
            tt('dve', modT[:, l], V(pb, 0, [[2, 48], [1, 2]]), V(bT, l * 48, [[1, 48], [0, 2]]), ALU.add,
               [('ps', l % 2), 'bT'], ['modT'])
            for sub in range(2):
                si = (1 + 3 * sub) * 8
                stt('dve', Gc[:, l, sub], modT[:, l, si:si + 8, :], 1.0,
                    V(nmT, (sub * 4 + l) * 8, [[1, 8], [0, 2]]), ALU.add, ALU.mult,
                    ['modT', 'nmT'], ['Gc'])

    def mcol(l, which, c, sidx):
        return modT[:, l, which * 8 + c, sidx:sidx + 1]

    def gcol(l, sub, c, sidx):
        return Gc[:, l, sub, c, sidx:sidx + 1]

    class NM:
        def __init__(self, T, nbuf_h=2):
            self.T = T
            self.x = [S.sb([128, 8, T], F32, f"x{i}") for i in range(2)]
            self.xsq = S.sb([128, 8, T], BF16, "xsq")
            self.t = S.sb([128, 8, T], F32, "tn")
            self.h = [S.sb([128, 8, T], BF16, f"h{i}") for i in range(nbuf_h)]
            self.rt = S.sb([128, T], F32, "rt")
            self.rstd = S.sb([128, T], F32, "rstd")
            self.nh = nbuf_h

        def load(self, i, src_d):
            T = self.T
            par = i % 2
            v = rows(src_d)[:, :, i * T:(i + 1) * T]
            dma(self.x[par][:, 0:4, :], v[:, 0:4, :], [('xd', i)], [('x', par)])
            dma(self.x[par][:, 4:8, :], v[:, 4:8, :], [('xd', i)], [('x', par)])

        def parts(self, i, l, sub, psb=6, pskey=None):
            T = self.T
            pskey = pskey or ('ps', psb)
            par = i % 2
            hp = i % self.nh
            sidx = 0 if i * T < NS else 1
            x = self.x[par]

            def pa():
                act(self.xsq[:], x[:], AF.Square, [('x', par)], ['xsq'])

            def pb():
                for c in range(8):
                    mm(ps[psb][:, 0:T], ones_bf[:], self.xsq[:, c, :], c == 0, c == 7, ['xsq', 'ones'], [pskey])

            def pc():
                act(self.rt[:], ps[psb][:, 0:T], AF.Ln, [pskey], ['rt'], bias=epsb[:, 0:1], scale=1.0 / 1024)
                act(self.rstd[:], self.rt[:], AF.Exp, ['rt'], ['rstd'], scale=-0.5)

            def pd():
                tt('dve', self.t[:], x[:], V(self.rstd, 0, [[0, 8], [1, T]]), ALU.mult, [('x', par), 'rstd'], ['tn'])

            def pe():
                for c in range(8):
                    act(self.h[hp][:, c, :], self.t[:, c, :], AF.Identity, ['tn', 'Gc', 'modT'], [('h', hp)],
                        bias=mcol(l, 3 * sub, c, sidx), scale=gcol(l, sub, c, sidx))
            return [pa, pb, pc, pd, pe], (self.h[hp], ('h', hp), sidx)

        def run(self, i, l, sub, psb=6, pskey=None):
            fs, res = self.parts(i, l, sub, psb, pskey)
            for f in fs:
                f()
            return res

    def phase_mlp(l, src_d, dst_d):
        S.phase = f'mlp{l}'
        mark = S.sb_mark()
        T = 512
        nt = NTOK // T
        wup = S.sb([128, 8, 4096], BF16, "wup")
        wdn = S.sb([128, 32, 1024], BF16, "wdn")
        for cg in range(8):
            cast_load(wup[:, :, cg * 512:(cg + 1) * 512], rows(D['w_up'][l])[:, :, cg * 512:(cg + 1) * 512], ('wup', cg))
        for cg in range(4):
            for kh in range(2):
                cast_load(wdn[:, kh * 16:(kh + 1) * 16, cg * 256:(cg + 1) * 256],
                          rows(D['w_down'][l])[:, kh * 16:(kh + 1) * 16, cg * 256:(cg + 1) * 256], ('wdn', cg, kh))
        xq = [S.sb([128, T], F32, f"xq{i}") for i in range(3)]
        tc = [S.sb([128, T], F32, f"tc{i}") for i in range(3)]
        xo = [S.sb([128, T], F32, f"xo{i}") for i in range(3)]
        sqc = [S.sb([128, T], BF16, f"sqc{i}") for i in range(2)]
        h = S.sb([128, 8, T], BF16, "hm")
        lnr = S.sb([128, T], F32, "lnr")
        rstd = S.sb([128, T], F32, "rstdm")
        a = S.sb([128, 32, T], BF16, "actb")
        rl = [S.sb([128, T], BF16, f"rl{i}") for i in range(2)]
        akeys = [('a', j) for j in range(32)]
        cn = {'q': 0, 't': 0, 'o': 0}

        def xchunk(i, c):
            return src_d[c * 128:(c + 1) * 128, i * T:(i + 1) * T]

        def na_load(i, c):
            r = (i * 8 + c) % 3
            dma(xq[r][:], xchunk(i, c), [('xd', i)], [('xq', r)])
            act(sqc[c % 2][:], xq[r][:], AF.Square, [('xq', r)], [('sqc', c % 2)])

        def na_mm(i, c):
            mm(ps[6][:], ones_bf[:], sqc[c % 2][:], c == 0, c == 7, [('sqc', c % 2), 'ones'], [('ps', 6)])

        def na_fin(i):
            act(lnr[:], ps[6][:], AF.Ln, [('ps', 6)], ['lnr'], bias=epsb[:, 0:1], scale=1.0 / 1024)
            act(rstd[:], lnr[:], AF.Exp, ['lnr'], ['rstdm'], scale=-0.5)

        def norm_a(i):
            for c in range(8):
                na_load(i, c)
                na_mm(i, c)
            na_fin(i)

        def norm_a_spread(i, n):
            c2 = n - 10
            if 0 <= c2 < 8:
                na_mm(i, c2)
            c1 = n - 8
            if 0 <= c1 < 8:
                na_load(i, c1)
            if n == 19:
                na_fin(i)

        def norm_b(i):
            sidx = 0 if i * T < NS else 1
            for c in range(8):
                r = cn['t'] % 3
                cn['t'] += 1
                dma(tc[r][:], xchunk(i, c), [('xd', i)], [('tc', r)])
                tt('dve', tc[r][:], tc[r][:], rstd[:], ALU.mult, [('tc', r), 'rstdm'], [('tc', r)])
                act(h[:, c, :], tc[r][:], AF.Identity, [('tc', r), 'Gc', 'modT'], ['hm'],
                    bias=mcol(l, 3, c, sidx), scale=gcol(l, 1, c, sidx))

        norm_a(0)
        norm_b(0)
        for i in range(nt):
            sidx = 0 if i * T < NS else 1
            for n in range(32):
                b = n % 3
                for k in range(8):
                    mm(ps[b][:], wup[:, k, n * 128:(n + 1) * 128], h[:, k, :], k == 0, k == 7, ['hm', ('wup', n // 4)], [('ps', b)])
                act(rl[n % 2][:], ps[b][:], AF.Relu, [('ps', b)], [('rl', n % 2)])
                tt('pool', a[:, n, :], rl[n % 2][:], rl[n % 2][:], ALU.mult, [('rl', n % 2)], [('a', n)])
                if i + 1 < nt:
                    norm_a_spread(i + 1, n)
            for m in range(8):
                b = 3 + m % 2
                r = cn['o'] % 3
                cn['o'] += 1
                dma(xo[r][:], xchunk(i, m), [('xd', i)], [('xo', r)])
                for k in range(32):
                    mm(ps[b][:], wdn[:, k, m * 128:(m + 1) * 128], a[:, k, :], k == 0, k == 31,
                       (akeys if k == 0 else []) + [('wdn', m // 2, k // 16)], [('ps', b)])
                stt('dve', xo[r][:], ps[b][:], mcol(l, 5, m, sidx), xo[r][:], ALU.mult, ALU.add,
                    [('ps', b), ('xo', r), 'modT'], [('xo', r)])
                dma(dst_d[m * 128:(m + 1) * 128, i * T:(i + 1) * T], xo[r][:], [('xo', r)], [('xdo', i, m)])
                if m == 1 and i + 1 < nt:
                    norm_b(i + 1)
        S.sb_reset(mark)

    def phase_outproj(l, nk, w_dram, src_feat_d, x_src, x_dst):
        S.phase = f'outproj{l}'
        mark = S.sb_mark()
        T = 512
        w = S.sb([128, nk, 1024], BF16, "wout")
        for k4 in range(nk // 4):
            cast_load(w[:, k4 * 4:(k4 + 1) * 4, :], rows(w_dram)[:, k4 * 4:(k4 + 1) * 4, :], 'wout')
        u = [S.sb([128, nk, T], BF16, f"u{i}") for i in range(2)]
        x = [S.sb([128, 8, T], F32, f"xr{i}") for i in range(2)]
        xo = [S.sb([128, 8, T], F32, f"xo{i}") for i in range(2)]
        nt = NTOK // T

        def load(i):
            par = i % 2
            dma(u[par][:], rows(src_feat_d)[:, :, i * T:(i + 1) * T], [], [('u', par)])
            dma(x[par][:], rows(x_src)[:, :, i * T:(i + 1) * T], [('xd', i)], [('x', par)])
        load(0)
        load(1)
        for i in range(nt):
            par = i % 2
            sidx = 0 if i * T < NS else 1
            for m in range(8):
                b = m % 4
                for k in range(nk):
                    mm(ps[b][:], w[:, k, m * 128:(m + 1) * 128], u[par][:, k, :], k == 0, k == nk - 1,
                       [('u', par), 'wout'], [('ps', b)])
                stt('dve', xo[par][:, m, :], ps[b][:], mcol(l, 2, m, sidx), x[par][:, m, :], ALU.mult, ALU.add,
                    [('ps', b), ('x', par), 'modT'], [('xo', par)])
            dma(rows(x_dst)[:, :, i * T:(i + 1) * T], xo[par][:], [('xo', par)], [('xd', i)])
            if i + 2 < nt:
                load(i + 2)
        S.sb_reset(mark)

    def phase_attn(l, slot, x_src, x_dst):
        S.phase = f'A1_{l}'
        mark0 = S.sb_mark()
        vext = S.sb([128, 36, 4, 66], BF16, "vext")
        kdup = S.sb([128, 4, NTOK], BF16, "kdup")
        S.op('pool', lambda e: e.memset(vext[:, :, :, 64:66], 1.0), [], ['vext1'])
        mark1 = S.sb_mark()
        T = 256
        w = S.sb([128, 8, 1792], BF16, "win")
        for k in range(8):
            cast_load(w[:, k, :], rows(D['attn_win'][slot])[:, k, :], 'win')
        cosA = S.sb([128, NS], F32, "cosA"); sinA = S.sb([128, NS], F32, "sinA")
        dma(cosA[:], D['cosA'], [], ['cosA']); dma(sinA[:], D['sinA'], [], ['sinA'])
        rot = S.sb([128, 128], BF16, "rot"); bdo = S.sb([128, 128], BF16, "bdo")
        cast_load(rot[:], D['rotA'], 'rot'); cast_load(bdo[:], D['bdones'], 'bdo')
        qg = S.sb([128, 2], F32, "qg"); kg = S.sb([128, 2], F32, "kg")
        dma(qg[:], D['qgT'], [], ['qg']); dma(kg[:], D['kgT'], [], ['kg'])
        nm = NM(T)
        qt = [S.sb([128, 8, T], BF16, f"qt{i}") for i in range(2)]
        sqb = [S.sb([128, 2, T], BF16, f"sqb{i}") for i in range(3)]
        rtb = [S.sb([128, 2, T], F32, f"rtb{i}") for i in range(3)]
        rsb = [S.sb([128, 2, T], F32, f"rsb{i}") for i in range(3)]
        qnb = [S.sb([128, 2, T], BF16, f"qnb{i}") for i in range(3)]
        t1b = [S.sb([128, 2, T], F32, f"t1b{i}") for i in range(3)]
        t2b = [S.sb([128, 2, T], F32, f"t2b{i}") for i in range(2)]
        kf = S.sb([128, 4, T], F32, "kf")
        vf = [S.sb([128, 256], F32, f"vf{i}") for i in range(2)]
        nt = NTOK // T
        nm.load(0, x_src); nm.load(1, x_src)
        hs = {}
        hs[0] = nm.run(0, l, 0, psb=7, pskey=('ps', 7))
        pend = [None]
        NIT = nt * 6

        def info(it):
            i, p = divmod(it, 6)
            t0 = i * T
            sample = t0 < NS
            return i, p, t0, sample, i % 2

        def pv3(b):
            return V(ps[b], 0, [[T, 2], [1, T]])

        def s0(it):
            i, p, t0, sample, par = info(it)
            if i + 1 < nt:
                if p == 0:
                    pend[0], hs[i + 1] = nm.parts(i + 1, l, 0, psb=7, pskey=('ps', 7))
                if p < 5:
                    pend[0][p]()
            h, hk, sidx = hs[i]
            b = it % 4
            for half in range(2):
                col0 = (2 * p + half) * 128
                for k in range(8):
                    mm(ps[b][:, half * T:(half + 1) * T], w[:, k, col0:col0 + 128], h[:, k, :], k == 0, k == 7,
                       [hk, 'win'], [('ps', b)])
            act(sqb[it % 3][:], pv3(b), AF.Square, [('ps', b)], [('sqb', it % 3)])

        def s2(it):
            i, p, t0, sample, par = info(it)
            nb_ = 4 + it % 3
            r3 = it % 3
            for half in range(2):
                mm(ps[nb_][:, half * T:(half + 1) * T], bdo[:], sqb[r3][:, half, :], True, True,
                   [('sqb', r3), 'bdo'], [('ps', nb_)])
            act(rtb[r3][:], pv3(nb_), AF.Ln, [('ps', nb_)], [('rtb', r3)], bias=epsb[:, 0:1], scale=1.0 / 64)
            act(rsb[r3][:], rtb[r3][:], AF.Exp, [('rtb', r3)], [('rsb', r3)], scale=-0.5)
            b = it % 4
            isq = p < 4
            gcolp = (qg if isq else kg)[:, slot:slot + 1]
            if sample:
                stt('dve', qnb[r3][:], pv3(b), gcolp, rsb[r3][:], ALU.mult, ALU.mult,
                    [('ps', b), ('rsb', r3), 'qg', 'kg'], [('qnb', r3)])
            else:
                if isq:
                    dst = qt[par][:, 2 * p:2 * p + 2, :]; dkey = ('qt', par)
                else:
                    g0 = 2 * (p - 4)
                    dst = kdup[:, g0:g0 + 2, t0:t0 + T]; dkey = ('kdup', i)
                stt('dve', dst, pv3(b), gcolp, rsb[r3][:], ALU.mult, ALU.mult,
                    [('ps', b), ('rsb', r3), 'qg', 'kg'], [dkey])
                if not isq:
                    stt('dve', kf[:, g0:g0 + 2, :], pv3(b), gcolp, rsb[r3][:], ALU.mult, ALU.mult,
                        [('ps', b), ('rsb', r3), 'qg', 'kg'], ['kf'])
                finish(it)

        def s4(it):
            i, p, t0, sample, par = info(it)
            if not sample:
                return
            nb_ = 4 + it % 3
            r3 = it % 3
            for half in range(2):
                mm(ps[nb_][:, half * T:(half + 1) * T], rot[:], qnb[r3][:, half, :], True, True,
                   [('qnb', r3), 'rot'], [('ps', nb_)])
            tt('pool', t1b[r3][:], qnb[r3][:], V(cosA, t0, [[0, 2], [1, T]]), ALU.mult, [('qnb', r3), 'cosA'], [('t1b', r3)])

        def s5(it):
            i, p, t0, sample, par = info(it)
            if not sample:
                return
            nb_ = 4 + it % 3
            r3 = it % 3
            tt('dve', t2b[it % 2][:], pv3(nb_), V(sinA, t0, [[0, 2], [1, T]]), ALU.mult, [('ps', nb_), 'sinA'], [('t2b', it % 2)])
            if p < 4:
                dst = qt[par][:, 2 * p:2 * p + 2, :]; dkey = ('qt', par)
            else:
                g0 = 2 * (p - 4)
                dst = kdup[:, g0:g0 + 2, t0:t0 + T]; dkey = ('kdup', i)
            tt('pool', dst, t1b[r3][:], t2b[it % 2][:], ALU.add, [('t1b', r3), ('t2b', it % 2)], [dkey])
            finish(it)

        def finish(it):
            i, p, t0, sample, par = info(it)
            if p == 3:
                dma(rows(D['qT_d'])[:, :, t0:t0 + T], qt[par][:], [('qt', par)], [('qd', i)])
            if p == 5 and not sample:
                pt0 = t0 - NS
                for g in range(4):
                    dma(D['k_out'][slot, g * 64:(g + 1) * 64, pt0:pt0 + T], kf[0:64, g, :], ['kf'], [('ko', i, g)])

        def sv(it):
            i, p, t0, sample, par = info(it)
            if p != 5:
                return
            h, hk, sidx = hs[i]
            for blk in range(T // 128):
                gb = (t0 // 128) + blk
                for k in range(8):
                    mm(ps[7][:, 256:512], h[:, k, blk * 128:(blk + 1) * 128], w[:, k, 1536:1792], k == 0, k == 7,
                       [hk, 'win'], [('ps', 7)])
                act(vext[:, gb, :, 0:64], V(ps[7], 256, [[64, 4], [1, 64]]), AF.Copy, [('ps', 7)], [('vext', gb)])
                if not sample:
                    act(vf[blk][:], ps[7][:, 256:512], AF.Copy, [('ps', 7)], [('vf', blk)])
                    r0 = t0 - NS + blk * 128
                    dma(D['v_out'][slot, r0:r0 + 128, :], vf[blk][:], [('vf', blk)], [('vo', gb)])
            if i + 2 < nt:
                nm.load(i + 2, x_src)

        pipeline(NIT, [(0, s0), (0, sv), (1, s2), (2, s4), (3, s5)])
        S.sb_reset(mark1)
        S.phase = f'A2_{l}'
        wo = S.sb([128, 8, 1024], BF16, "wo")
        for k4 in range(2):
            cast_load(wo[:, k4 * 4:(k4 + 1) * 4, :], rows(D['attn_wout'][slot])[:, k4 * 4:(k4 + 1) * 4, :], 'wo')
        kz1 = S.sb([128, 4, NTOK], BF16, "kz1")
        S.op('pool', lambda e: e.memset(kz1[0:64, :, :], 0.0), [], ['kz1a'])
        act(kz1[64:128, :, :], kdup[64:128, :, :], AF.Copy, ['kdupall'], ['kz1b'])
        S.op('pool', lambda e: e.memset(kdup[64:128, :, :], 0.0), [], ['kdupall', 'kz0'])
        kz = [kdup, kz1]
        kzk = [['kz0'], ['kz1a', 'kz1b']]
        kcz = [S.sb([128, 4, 512], BF16, f"kcz{i}") for i in range(2)]
        S.op('pool', lambda e: e.memset(kcz[0][64:128, :, :], 0.0), [], ['kcz0z'])
        S.op('pool', lambda e: e.memset(kcz[1][0:64, :, :], 0.0), [], ['kcz1z'])
        cast_load(kcz[0][0:64, :, :], D['kctx'][slot][0:64], 'kcz0')
        cast_load(kcz[1][64:128, :, :], D['kctx'][slot][64:128], 'kcz1')
        kczk = [['kcz0', 'kcz0z'], ['kcz1', 'kcz1z']]
        vctx = S.sb([128, 4, 4, 66], BF16, "vctx")
        S.op('pool', lambda e: e.memset(vctx[:, :, :, 64:66], 1.0), [], ['vctx1'])
        for j in range(4):
            cast_load(vctx[:, j, :, 0:64], D['vctx'][slot][:, j], ('vctx', j))
        mlo = S.sb([128, 128], BF16, "mlo"); mhi = S.sb([128, 128], BF16, "mhi")
        cast_load(mlo[:], D['mlo'], 'mlo'); cast_load(mhi[:], D['mhi'], 'mhi')
        sk = S.sb([128, 16], F32, "sk"); esk = S.sb([128, 16], F32, "esk")
        dma(sk[:], D['sinkB'][:, slot, :], [], ['sk'])
        act(esk[:], sk[:], AF.Exp, ['sk'], ['esk'])
        TQ = 512
        qs = [S.sb([128, 8, TQ], BF16, f"qs{i}") for i in range(2)]
        xs = [S.sb([128, 8, TQ], F32, f"xs{i}") for i in range(2)]
        oT = [S.sb([128, 8, TQ], BF16, f"oT{i}") for i in range(2)]
        osb = [S.sb([128, 1024], BF16, f"osb{i}") for i in range(2)]
        NPT = 5
        PT = [S.sb([128, 7, 128], BF16, f"PT{i}") for i in range(NPT)]
        den = [S.sb([128, 4], F32, f"den{i}") for i in range(2)]
        rec = [S.sb([128, 4], F32, f"rec{i}") for i in range(2)]
        ng = NTOK // TQ

        def load2(gi):
            par = gi % 2
            dma(qs[par][:], rows(D['qT_d'])[:, :, gi * TQ:(gi + 1) * TQ], [('qd', 2 * gi), ('qd', 2 * gi + 1)], [('qs', par)])
            dma(xs[par][:], rows(x_src)[:, :, gi * TQ:(gi + 1) * TQ], [('xd', gi)], [('xs', par)])
        load2(0); load2(1)
        NITEM = 36 * 16
        ring = [0]
        plan = {}

        def keys_of(n, hh):
            g = hh // 4
            hp = hh % 2
            keys = []
            if n < 32:
                for j in range(4):
                    keys.append((kcz[hp][:, g, j * 128:(j + 1) * 128], vctx[:, j, g, 0:65], None,
                                 kczk[hp], [('vctx', j), 'vctx1']))
                for nn_, mk in ((n - 1, mlo), (n, None), (n + 1, mhi)):
                    if 0 <= nn_ < 32:
                        keys.append((kz[hp][:, g, nn_ * 128:(nn_ + 1) * 128], vext[:, nn_, g, 0:65], mk,
                                     kzk[hp], [('vext', nn_), 'vext1']))
            else:
                sq0 = 32 + ((n - 32) // 2) * 2
                for nn_ in (sq0, sq0 + 1):
                    keys.append((kz[hp][:, g, nn_ * 128:(nn_ + 1) * 128], vext[:, nn_, g, 0:65], None,
                                 kzk[hp], [('vext', nn_), 'vext1']))
            return keys

        def a0(it):
            n, hh = divmod(it, 16)
            gi, bq = divmod(n, 4)
            gpar = gi % 2
            c = hh // 2
            pr0 = (hh % 2) * 64
            keys = keys_of(n, hh)
            nk = len(keys)
            nbank = (nk + 3) // 4
            banks = [(ring[0] + j) % 5 for j in range(nbank)]
            ring[0] = (ring[0] + nbank) % 5
            plan[it] = (keys, banks)
            pt = PT[it % NPT]
            ptk = ('PT', it % NPT)
            for idx, (kT, vE, mk, kk, vk) in enumerate(keys):
                bk = banks[idx // 4]
                mm(ps[bk][:, (idx % 4) * 128:(idx % 4 + 1) * 128], kT,
                   qs[gpar][:, c, bq * 128:(bq + 1) * 128], True, True,
                   kk + [('qs', gpar)], [('ps', bk)])
            for j, bk in enumerate(banks):
                n0 = min(nk - 4 * j, 4)
                act(pt[:, 4 * j:4 * j + n0, :], V(ps[bk], 0, [[128, n0], [1, 128]]), AF.Exp, [('ps', bk)], [ptk], scale=0.125)
            for idx, (kT, vE, mk, kk, vk) in enumerate(keys):
                if mk is not None:
                    tt('pool', pt[:, idx, :], pt[:, idx, :], mk[:], ALU.mult, [ptk, 'mlo', 'mhi'], [ptk])

        def a1(it):
            n, hh = divmod(it, 16)
            gi, bq = divmod(n, 4)
            g = hh // 4
            keys, banks = plan.pop(it)
            nk = len(keys)
            pt = PT[it % NPT]
            ptk = ('PT', it % NPT)
            opar = n % 2
            q4 = it // 4
            ob = 5 + q4 % 2
            oc = (hh % 4) * 128
            for idx, (kT, vE, mk, kk, vk) in enumerate(keys):
                mm(ps[ob][:, oc:oc + 65], pt[:, idx, :], vE, idx == 0, idx == nk - 1, [ptk] + vk, [('ps', ob)])
            if hh % 4 == 3:
                d_ = den[q4 % 2]; r_ = rec[q4 % 2]
                dk = ('den', q4 % 2); rk = ('rec', q4 % 2)
                tt('dve', d_[:], V(ps[ob], 64, [[128, 4]]), esk[:, 4 * g:4 * g + 4], ALU.add, [('ps', ob), 'esk'], [dk])
                S.op('dve', lambda e, d_=d_, r_=r_: e.reciprocal(r_[:], d_[:]), [dk], [rk])
                tt('dve', V(osb[opar], g * 256, [[64, 4], [1, 64]]), V(ps[ob], 0, [[128, 4], [1, 64]]),
                   V(r_, 0, [[1, 4], [0, 64]]), ALU.mult, [('ps', ob), rk], [('osb', opar)])

        def a2(it):
            n, hh = divmod(it, 16)
            if hh != 15:
                return
            gi, bq = divmod(n, 4)
            gpar = gi % 2
            opar = n % 2
            sidx = 0 if gi * TQ < NS else 1
            for c in range(8):
                S.op('pe', lambda e, c=c, opar=opar: e.transpose(psbf[:, c * 128:(c + 1) * 128],
                                                                  osb[opar][:, c * 128:(c + 1) * 128], ident_bf[:]),
                     [('osb', opar), 'ident'], [('ps', 7)])
            act(oT[gpar][:, :, bq * 128:(bq + 1) * 128], psbf3(8), AF.Copy, [('ps', 7)], [('oT', gpar)])
            if bq == 3:
                for m in range(8):
                    for k in range(8):
                        mm(ps[7][:], wo[:, k, m * 128:(m + 1) * 128], oT[gpar][:, k, :], k == 0, k == 7,
                           [('oT', gpar), 'wo'], [('ps', 7)])
                    stt('dve', xs[gpar][:, m, :], ps[7][:], mcol(l, 2, m, sidx), xs[gpar][:, m, :], ALU.mult, ALU.add,
                        [('ps', 7), ('xs', gpar), 'modT'], [('xs', gpar)])
                dma(rows(x_dst)[:, :, gi * TQ:(gi + 1) * TQ], xs[gpar][:], [('xs', gpar)], [('xd', gi)])
                if gi + 2 < ng:
                    load2(gi + 2)

        pipeline(NITEM, [(0, a0), (3, a1), (5, a2)])
        S.sb_reset(mark0)

    def phase_ret(l, x_src, x_dst):
        T = 256
        nt = NTOK // T
        S.phase = 'R1a'
        mark = S.sb_mark()
        T = 512
        nt = NTOK // T
        w = S.sb([128, 8, 2048], BF16, "rwin")
        for k in range(8):
            cast_load(w[:, k, :], rows(D['ret_win'])[:, k, 0:2048], 'rwin')
        cosR = S.sb([128, NS], F32, "cosR"); sinR = S.sb([128, NS], F32, "sinR")
        dma(cosR[:], D['cosR'], [], ['cosR']); dma(sinR[:], D['sinR'], [], ['sinR'])
        nm = NM(T)
        qk = [S.sb([128, 16, T], BF16, "qk0")] * 2
        ktk = [S.sb([128, T // 128, 1024], BF16, "ktk0")] * 2
        tm = [[S.sb([128, T], F32, f"tm{a}{i}") for i in range(2)] for a in range(4)]
        nm.load(0, x_src); nm.load(1, x_src)
        cnt = 0
        hs = {0: nm.run(0, l, 0, psb=6)}
        for i in range(nt):
            par = 0
            t0 = i * T
            h, hk, sidx = hs.pop(i)
            sample = sidx == 0
            for pi in range(8):
                if pi == 4 and i + 1 < nt:
                    hs[i + 1] = nm.run(i + 1, l, 0, psb=6)
                cA = pi * 2
                u = cnt % 2
                cnt += 1
                pA = ps[u]; pB = ps[2 + u]
                for half, pp in ((0, pA), (1, pB)):
                    col0 = (cA + half) * 128
                    for k in range(8):
                        mm(pp[:, 0:T], w[:, k, col0:col0 + 128], h[:, k, :], k == 0, k == 7, [hk, 'rwin'],
                           [('ps', u + 2 * half)])
                dA = qk[par][:, cA, :]; dB = qk[par][:, cA + 1, :]
                if sample:
                    cs = cosR[:, t0:t0 + T]; sn = sinR[:, t0:t0 + T]
                    tt('dve', tm[0][u][:], pA[:, 0:T], cs, ALU.mult, [('ps', u), 'cosR'], [('tm0', u)])
                    tt('dve', tm[1][u][:], pB[:, 0:T], sn, ALU.mult, [('ps', 2 + u), 'sinR'], [('tm1', u)])
                    tt('dve', tm[2][u][:], pB[:, 0:T], cs, ALU.mult, [('ps', 2 + u), 'cosR'], [('tm2', u)])
                    tt('dve', tm[3][u][:], pA[:, 0:T], sn, ALU.mult, [('ps', u), 'sinR'], [('tm3', u)])
                    tt('pool', dA, tm[0][u][:], tm[1][u][:], ALU.subtract, [('tm0', u), ('tm1', u)], [('qk', par)])
                    tt('pool', dB, tm[2][u][:], tm[3][u][:], ALU.add, [('tm2', u), ('tm3', u)], [('qk', par)])
                else:
                    act(dA, pA[:, 0:T], AF.Copy, [('ps', u)], [('qk', par)])
                    act(dB, pB[:, 0:T], AF.Copy, [('ps', 2 + u)], [('qk', par)])
            for blk in range(T // 128):
                for c in range(8):
                    S.op('pe', lambda e, c=c, blk=blk, par=par: e.transpose(
                        psbf[:, c * 128:(c + 1) * 128], qk[par][:, 8 + c, blk * 128:(blk + 1) * 128], ident_bf[:]),
                        [('qk', par), 'ident'], [('ps', 7)])
                act(ktk[par][:, blk, :], psbf, AF.Copy, [('ps', 7)], [('ktk', par)])
            dma(rows(D['rq_d'])[:, :, t0:t0 + T], qk[par][:, 0:8, :], [('qk', par)], [('rqd', i)])
            dma(rows(D['rk_d'])[:, :, t0:t0 + T], qk[par][:, 8:16, :], [('qk', par)], [('rkd', i)])
            dma(D['ktok_d'][t0:t0 + T, :].rearrange("(b p) n -> p b n", p=128), ktk[par][:], [('ktk', par)], [('ktd', i)])
            if i + 2 < nt:
                nm.load(i + 2, x_src)
        S.sb_reset(mark)
        S.phase = 'R1b'
        T = 256
        nt = NTOK // T
        w2 = S.sb([128, 8, 4096], BF16, "rwin2")
        for k in range(8):
            for hh in range(2):
                cast_load(w2[:, k, hh * 2048:(hh + 1) * 2048], rows(D['ret_win'])[:, k, 2048 + hh * 2048:2048 + (hh + 1) * 2048], 'rwin2')
        gB = S.sb([128, 2048], F32, "gB")
        dma(gB[:], D['gainB'], [], ['gB'])
        nm = NM(T)
        vt = [S.sb([128, 2, 2048], BF16, f"vt{i}") for i in range(2)]
        sgt = [S.sb([128, 2, 2048], BF16, f"sgt{i}") for i in range(2)]
        sgf = [S.sb([128, 512], F32, f"sgf{i}") for i in range(2)]
        nm.load(0, x_src); nm.load(1, x_src)
        cnt = 0
        hs = {0: nm.run(0, l, 0, psb=6)}
        for i in range(nt):
            par = i % 2
            t0 = i * T
            h, hk, sidx = hs.pop(i)
            for blk in range(2):
                if blk == 1 and i + 1 < nt:
                    hs[i + 1] = nm.run(i + 1, l, 0, psb=6)
                for grp in range(8):
                    b = cnt % 4
                    cnt += 1
                    for k in range(8):
                        mm(ps[b][:], h[:, k, blk * 128:(blk + 1) * 128], w2[:, k, grp * 512:(grp + 1) * 512],
                           k == 0, k == 7, [hk, 'rwin2'], [('ps', b)])
                    if grp < 4:
                        S.op('dve', lambda e, par=par, blk=blk, grp=grp, b=b: e.tensor_copy(vt[par][:, blk, grp * 512:(grp + 1) * 512], ps[b][:]),
                             [('ps', b)], [('vt', par)])
                    else:
                        u = cnt % 2
                        act(sgf[u][:], ps[b][:], AF.Silu, [('ps', b)], [('sgf', u)])
                        tt('pool', sgt[par][:, blk, (grp - 4) * 512:(grp - 3) * 512], sgf[u][:],
                           gB[:, (grp - 4) * 512:(grp - 3) * 512], ALU.mult, [('sgf', u), 'gB'], [('sgt', par)])
            dma(D['vtok_d'][t0:t0 + T, :].rearrange("(b p) n -> p b n", p=128), vt[par][:], [('vt', par)], [('vtd', i)])
            dma(D['sg_d'][t0:t0 + T, :].rearrange("(b p) n -> p b n", p=128), sgt[par][:], [('sgt', par)], [('sgd', i)])
            if i + 2 < nt:
                nm.load(i + 2, x_src)
        S.sb_reset(mark)
        S.phase = 'R2'
        ld = S.sb([128, 8], F32, "ld")
        dma(ld[:], D['ldB'], [], ['ld'])
        gC = S.sb([128, 8], F32, "gC")
        act(gC[:], ld[:], AF.Exp, ['ld'], ['gC'], scale=128.0)
        Dc = S.sb([128, 4, 128], F32, "Dc")
        wq = S.sb([128, 2, 4, 128], F32, "wq")
        wst = S.sb([128, 2, 4], F32, "wst")
        cmark = S.sb_mark()
        cm = {}
        for nme in ('MfT', 'MbT', 'iq1', 'iqb', 'mhi', 'mgt'):
            cm[nme] = S.sb([128, 128], F32, nme)
            dma(cm[nme][:], D[nme], [], [nme])
        cst = S.sb([128, 2], F32, "cst")
        dma(cst[:, 0:1], D['cstf'], [], ['cst']); dma(cst[:, 1:2], D['cstb'], [], ['cst'])
        e1 = S.sb([128, 128], F32, "e1"); e2 = S.sb([128, 128], F32, "e2")
        for hd in range(4):
            lf = ld[:, hd:hd + 1]; lb = ld[:, 4 + hd:5 + hd]
            act(e1[:], cm['MfT'][:], AF.Exp, ['MfT', 'ld'], ['e1'], scale=lf)
            tt('dve', e1[:], e1[:], cm['mhi'][:], ALU.mult, ['e1', 'mhi'], ['e1'])
            act(e2[:], cm['MbT'][:], AF.Exp, ['MbT', 'ld'], ['e2'], scale=lb)
            tt('dve', e2[:], e2[:], cm['mgt'][:], ALU.mult, ['e2', 'mgt'], ['e2'])
            tt('dve', e1[:], e1[:], e2[:], ALU.add, ['e1', 'e2'], ['e1'])
            ts1('dve', Dc[:, hd, :], e1[:], 0.0625, ALU.mult, ['e1'], ['Dc'])
            act(wq[:, 0, hd, :], cm['iq1'][:], AF.Exp, ['iq1', 'ld'], ['wq'], scale=lf)
            act(wq[:, 1, hd, :], cm['iqb'][:], AF.Exp, ['iqb', 'ld'], ['wq'], scale=lb)
            act(wst[:, 0, hd:hd + 1], cst[:, 0:1], AF.Exp, ['cst', 'ld'], ['wst'], scale=lf, bias=lnb[:, 0:1])
            act(wst[:, 1, hd:hd + 1], cst[:, 1:2], AF.Exp, ['cst', 'ld'], ['wst'], scale=lb, bias=lnb[:, 0:1])
        S.sb_reset(cmark)
        qTh = S.sb([128, 2, NS], BF16, "qTh"); kTh = S.sb([128, 2, NS], BF16, "kTh")
        kth = S.sb([128, 32, 256], BF16, "kth"); vth = S.sb([128, 32, 512], BF16, "vth")
        Sst = S.sb([128, 32, 2, 512], BF16, "Sst")
        Sf = [S.sb([128, 2, 512], F32, f"Sf{i}") for i in range(2)]
        Sb = [S.sb([128, 2, 512], F32, f"Sb{i}") for i in range(2)]
        Sfb = [S.sb([128, 2, 512], BF16, f"Sfb{i}") for i in range(4)]
        kb = [S.sb([128, 256], BF16, f"kb{i}") for i in range(2)]
        kf2 = [S.sb([128, 256], BF16, f"kf{i}") for i in range(2)]
        PTr = [S.sb([128, 128], BF16, f"PTr{i}") for i in range(3)]
        NQF = 5
        qf = [S.sb([128, 2, 2, 128], BF16, f"qf{i}") for i in range(NQF)]
        st6 = [S.sb([128, 6], F32, f"st6{i}") for i in range(2)]
        mv = [S.sb([128, 2], F32, f"mv{i}") for i in range(4)]
        lnv = [S.sb([128, 1], F32, f"lnv{i}") for i in range(2)]
        rsd = [S.sb([128, 1], F32, f"rsd{i}") for i in range(3)]
        NOS = 5
        Osb = [S.sb([128, 512], F32, f"Osb{i}") for i in range(NOS)]
        NSG = 5
        sgc = [S.sb([128, 512], BF16, f"sgc{i}") for i in range(NSG)]
        uu = [S.sb([128, 512], BF16, f"uu{i}") for i in range(2)]
        uT = [S.sb([128, 4, 128], BF16, f"uT{i}") for i in range(2)]
        seqs = [(0, 32, True, -1), (32, 2, False, 0), (34, 2, False, 1)]
        for (b0, nb, smp, sq_i) in seqs:
            L = nb * 128
            c0 = b0 * 128
            for hd in range(4):
                dma(kth[:, 0:nb, :], D['ktok_d'][c0:c0 + L, hd * 256:(hd + 1) * 256].rearrange("(b p) n -> p b n", p=128), [], ['kth'])
                dma(vth[:, 0:nb, :], D['vtok_d'][c0:c0 + L, hd * 512:(hd + 1) * 512].rearrange("(b p) n -> p b n", p=128), [], ['vth'])
                dma(qTh[:, :, 0:L], rows(D['rq_d'])[:, 2 * hd:2 * hd + 2, c0:c0 + L], [], ['qTh'])
                dma(kTh[:, :, 0:L], rows(D['rk_d'])[:, 2 * hd:2 * hd + 2, c0:c0 + L], [], ['kTh'])
                if smp:
                    dma(Sf[0][:], D['sret'][0, hd], [], [('Sf', 0)])
                    dma(Sb[0][:], D['sret'][1, hd], [], [('Sb', 0)])
                else:
                    S.op('pool', lambda e: e.memset(Sf[0][:], 0.0), [], [('Sf', 0)])
                    S.op('pool', lambda e: e.memset(Sb[0][:], 0.0), [], [('Sb', 0)])
                act(Sst[:, nb - 1], Sb[0][:], AF.Copy, [('Sb', 0)], [('Sst', nb - 1)])
                lo = 1 if smp else 0
                nbw = nb - lo

                def B0(j):
                    n = nb - 1 - j
                    r3 = j % 2
                    act(kb[r3][:], kth[:, n, :], AF.Identity, ['kth', 'wst'], [('kb', r3)], scale=wst[:, 1, hd:hd + 1])

                def B1(j):
                    n = nb - 1 - j
                    r3 = j % 2
                    st_ = 2 * (j % 2)
                    for dc in range(2):
                        mm(ps[st_ + dc][:], kb[r3][:, dc * 128:(dc + 1) * 128], vth[:, n, :], True, True,
                           [('kb', r3), 'vth'], [('ps', st_ + dc)])

                def B2(j):
                    st_ = 2 * (j % 2)
                    src = Sb[j % 2]; dst = Sb[(j + 1) % 2]
                    for dc in range(2):
                        stt('dve', dst[:, dc, :], src[:, dc, :], gC[:, 4 + hd:5 + hd], ps[st_ + dc][:], ALU.mult, ALU.add,
                            [('Sb', j % 2), 'gC', ('ps', st_ + dc)], [('Sb', (j + 1) % 2)])

                def B3(j):
                    n = nb - 1 - j
                    dst = Sb[(j + 1) % 2]
                    if n >= 1:
                        act(Sst[:, n - 1], dst[:], AF.Copy, [('Sb', (j + 1) % 2)], [('Sst', n - 1)])
                    else:
                        dma(D['ret_out'][sq_i, 1, hd], dst[:], [('Sb', (j + 1) % 2)], [('ro', sq_i, 1, hd)])

                pipeline(nbw, [(0, B0), (1, B1), (2, B2), (3, B3)], desc=True)
                act(Sfb[0][:], Sf[0][:], AF.Copy, [('Sf', 0)], [('Sfb', 0)])

                def need_kv(n):
                    return (n < nb - 1) or (not smp)

                def F0(n):
                    cols = slice(n * 128, (n + 1) * 128)
                    if need_kv(n):
                        act(kf2[n % 2][:], kth[:, n, :], AF.Identity, ['kth', 'wst'], [('kf2', n % 2)], scale=wst[:, 0, hd:hd + 1])
                    for dr in range(2):
                        tt('pool', qf[n % NQF][:, dr], qTh[:, :, cols], V(wq, (dr * 4 + hd) * 128, [[0, 2], [1, 128]]), ALU.mult,
                           ['qTh', 'wq'], [('qf', n % NQF)])

                def P1(n):
                    cols = slice(n * 128, (n + 1) * 128)
                    if need_kv(n):
                        st_ = 2 * (n % 2)
                        for dc in range(2):
                            mm(ps[st_ + dc][:], kf2[n % 2][:, dc * 128:(dc + 1) * 128], vth[:, n, :], True, True,
                               [('kf2', n % 2), 'vth'], [('ps', st_ + dc)])
                    mm(ps[4][:, 0:128], kTh[:, 0, cols], qTh[:, 0, cols], True, False, ['kTh', 'qTh'], [('ps', 4)])
                    mm(ps[4][:, 0:128], kTh[:, 1, cols], qTh[:, 1, cols], False, True, ['kTh', 'qTh'], [('ps', 4)])

                def D2(n):
                    r3 = n % 3
                    tt('dve', PTr[r3][:], ps[4][:, 0:128], Dc[:, hd, :], ALU.mult, [('ps', 4), 'Dc'], [('PTr', r3)])
                    if need_kv(n):
                        st_ = 2 * (n % 2)
                        src = Sf[n % 2]; dst = Sf[(n + 1) % 2]
                        for dc in range(2):
                            stt('dve', dst[:, dc, :], src[:, dc, :], gC[:, hd:hd + 1], ps[st_ + dc][:], ALU.mult, ALU.add,
                                [('Sf', n % 2), 'gC', ('ps', st_ + dc)], [('Sf', (n + 1) % 2)])

                def A3(n):
                    if need_kv(n):
                        dst = Sf[(n + 1) % 2]
                        if n < nb - 1:
                            act(Sfb[(n + 1) % 4][:], dst[:], AF.Copy, [('Sf', (n + 1) % 2)], [('Sfb', (n + 1) % 4)])
                        else:
                            dma(D['ret_out'][sq_i, 0, hd], dst[:], [('Sf', (n + 1) % 2)], [('ro', sq_i, 0, hd)])

                def P4(n):
                    r3 = n % 3
                    ob = 5 + n % 2
                    sfb = Sfb[n % 4]
                    q_ = qf[n % NQF]
                    mm(ps[ob][:], PTr[r3][:], vth[:, n, :], True, False, [('PTr', r3), 'vth'], [('ps', ob)])
                    for dc in range(2):
                        mm(ps[ob][:], q_[:, 0, dc, :], sfb[:, dc, :], False, False, [('qf', n % NQF), ('Sfb', n % 4)], [('ps', ob)])
                    for dc in range(2):
                        mm(ps[ob][:], q_[:, 1, dc, :], Sst[:, n, dc, :], False, dc == 1, [('qf', n % NQF), ('Sst', n)], [('ps', ob)])

                def A5(n):
                    ob = 5 + n % 2
                    dma(sgc[n % NSG][:], D['sg_d'][(b0 + n) * 128:(b0 + n + 1) * 128, hd * 512:(hd + 1) * 512], [], [('sgc', n % NSG)])
                    act(Osb[n % NOS][:], ps[ob][:], AF.Copy, [('ps', ob)], [('Osb', n % NOS)])

                def D6(n):
                    u = n % 2
                    S.op('dve', lambda e, u=u, n=n: e.bn_stats(st6[u][:], Osb[n % NOS][:]), [('Osb', n % NOS)], [('st6', u)])
                    S.op('dve', lambda e, u=u, n=n: e.bn_aggr(mv[n % 4][:], st6[u][:]), [('st6', u)], [('mv', n % 4)])

                def A7(n):
                    u = n % 2
                    act(lnv[u][:], mv[n % 4][:, 1:2], AF.Ln, [('mv', n % 4)], [('lnv', u)], bias=epsb[:, 0:1], scale=1.0)
                    act(rsd[n % 3][:], lnv[u][:], AF.Exp, [('lnv', u)], [('rsd', n % 3)], scale=-0.5)

                def D8(n):
                    o_ = Osb[n % NOS]
                    ts('dve', o_[:], o_[:], mv[n % 4][:, 0:1], rsd[n % 3][:], ALU.subtract, ALU.mult,
                       [('Osb', n % NOS), ('mv', n % 4), ('rsd', n % 3)], [('Osb', n % NOS)])

                def Q9(n):
                    tt('pool', uu[n % 2][:], Osb[n % NOS][:], sgc[n % NSG][:], ALU.mult,
                       [('Osb', n % NOS), ('sgc', n % NSG)], [('uu', n % 2)])

                def P10(n):
                    for e_ in range(4):
                        S.op('pe', lambda e, e_=e_, n=n: e.transpose(psbf[:, e_ * 128:(e_ + 1) * 128],
                                                                     uu[n % 2][:, e_ * 128:(e_ + 1) * 128], ident_bf[:]),
                             [('uu', n % 2), 'ident'], [('ps', 7)])

                def A11(n):
                    u = n % 2
                    act(uT[u][:], psbf3(4), AF.Copy, [('ps', 7)], [('uT', u)])
                    dma(rows(D['uT_d'])[:, hd * 4:(hd + 1) * 4, (b0 + n) * 128:(b0 + n + 1) * 128], uT[u][:],
                        [('uT', u)], [('uTd', b0 + n, hd)])

                pipeline(nb, [(0, F0), (1, P1), (2, D2), (3, A3), (4, P4), (5, A5), (6, D6), (7, A7), (8, D8),
                              (9, Q9), (10, P10), (11, A11)], desc=True)
        S.sb_reset(mark)
        phase_outproj(l, 16, D['ret_wout'], D['uT_d'], x_src, x_dst)

    def phase_lru(l, x_src, x_dst):
        S.phase = 'L1'
        mark = S.sb_mark()
        T = 512
        nt = NTOK // T
        w = S.sb([128, 8, 2048], BF16, "lwin")
        for k in range(8):
            cast_load(w[:, k, :], rows(D['lru_win'])[:, k, :], 'lwin')
        nm = NM(T)
        gt = [S.sb([128, 8, T], BF16, f"gt{i}") for i in range(2)]
        xr = [S.sb([128, 8, T], F32, f"xrt{i}") for i in range(2)]
        nm.load(0, x_src); nm.load(1, x_src)
        cnt = 0
        hs = {0: nm.run(0, l, 0, psb=6)}
        for i in range(nt):
            par = i % 2
            h, hk, sidx = hs.pop(i)
            for c in range(8):
                if i + 1 < nt:
                    if c == 1:
                        pend, hs[i + 1] = nm.parts(i + 1, l, 0, psb=6)
                    if 1 <= c <= 5:
                        pend[c - 1]()
                b = cnt % 4
                cnt += 1
                for k in range(8):
                    mm(ps[b][:], w[:, k, c * 128:(c + 1) * 128], h[:, k, :], k == 0, k == 7, [hk, 'lwin'], [('ps', b)])
                act(gt[par][:, c, :], ps[b][:], AF.Gelu_apprx_tanh, [('ps', b)], [('gt', par)])
                b = cnt % 4
                cnt += 1
                for k in range(8):
                    mm(ps[b][:], w[:, k, 1024 + c * 128:1024 + (c + 1) * 128], h[:, k, :], k == 0, k == 7, [hk, 'lwin'], [('ps', b)])
                S.op('dve', lambda e, par=par, c=c, b=b: e.tensor_copy(xr[par][:, c, :], ps[b][:]), [('ps', b)], [('xrt', par)])
            dma(rows(D['gate_d'])[:, :, i * T:(i + 1) * T], gt[par][:], [('gt', par)], [('gd', i)])
            dma(rows(D['xr_d'])[:, :, i * T:(i + 1) * T], xr[par][:], [('xrt', par)], [('xrd', i)])
            if i + 2 < nt:
                nm.load(i + 2, x_src)
        S.sb_reset(mark)
        S.phase = 'L2'
        cv = S.sb([128, 5, 8], F32, "cv"); dma(cv[:], D['convT'], [], ['cv'])
        wr = S.sb([128, 2, 8, 128], BF16, "wr"); cast_load(wr[:], D['lru_wr'], 'wr')
        wi = S.sb([128, 2, 8, 128], BF16, "wi"); cast_load(wi[:], D['lru_wi'], 'wi')
        bT = S.sb([128, 3, 2, 8], F32, "lbT"); dma(bT[:], D['lru_bT'], [], ['lbT'])
        h0 = S.sb([128, 2, 8], F32, "h0"); dma(h0[:], D['slru'], [], ['h0'])
        clam = S.sb([128, 2, 8], F32, "clam"); ctmp = S.sb([128, 2, 8], F32, "ctmp")
        act(ctmp[:], bT[:, 2], AF.Exp, ['lbT'], ['ctmp'], scale=-1.0)
        act(ctmp[:], ctmp[:], AF.Ln, ['ctmp'], ['ctmp'], bias=oneb[:, 0:1])
        ts1('dve', clam[:], ctmp[:], -8.0, ALU.mult, ['ctmp'], ['clam'])
        XR = [S.sb([128, NTOK], F32, f"XR{i}") for i in range(2)]
        G = [S.sb([128, NTOK], BF16, f"G{i}") for i in range(2)]
        XC = S.sb([128, NTOK], F32, "XC")
        RAd = [S.sb([128, NTOK], F32, f"RA{i}") for i in range(2)]
        IUd = [S.sb([128, NTOK], F32, f"IU{i}") for i in range(2)]
        HF = S.sb([128, NTOK], F32, "HF"); HB = S.sb([128, NTOK], F32, "HB")
        XCB = S.sb([128, NTOK], BF16, "XCB")
        stc = S.sb([128, 2, 2], F32, "stc")
        segs = [(0, NS, True), (NS, 256, False), (NS + 256, 256, False)]
        xck = [('XC', 0), ('XC', 1), ('XC', 2)]
        hfk = [('HF', 0), ('HF', 1), ('HF', 2)]
        hbk = [('HB', 0), ('HB', 1), ('HB', 2)]

        def loadc(c):
            p = c % 2
            dma(XR[p][:], D['xr_d'][c * 128:(c + 1) * 128, :], [], [('XR', p)])
            dma(G[p][:], D['gate_d'][c * 128:(c + 1) * 128, :], [], [('G', p)])
        loadc(0)
        cnt = 0
        for c in range(8):
            p = c % 2
            if c + 1 < 8:
                loadc(c + 1)
            act(XC[:], XR[p][:], AF.Identity, [('XR', p), 'cv'], xck, scale=cv[:, 2, c:c + 1], bias=cv[:, 4, c:c + 1])
            for si, (s0, L, smp) in enumerate(segs):
                eng = 'dve'
                for (tap, do, di, n_) in ((0, 2, 0, L - 2), (1, 1, 0, L - 1), (3, 0, 1, L - 1)):
                    stt(eng, XC[:, s0 + do:s0 + do + n_], XR[p][:, s0 + di:s0 + di + n_], cv[:, tap, c:c + 1],
                        XC[:, s0 + do:s0 + do + n_], ALU.mult, ALU.add, [('XR', p), 'cv', ('XC', si)], [('XC', si)])
            act(XCB[:], XC[:], AF.Copy, xck, ['XCB'])
            for dr in range(2):
                RA = RAd[dr]; IU = IUd[dr]
                TMP = HF if dr == 0 else HB
                tk = hfk if dr == 0 else hbk
                for ti in range(nt):
                    cols = slice(ti * T, (ti + 1) * T)
                    b = cnt % 4
                    cnt += 1
                    mm(ps[b][:], wr[:, dr, c, :], XCB[:, cols], True, True, ['XCB', 'wr'], [('ps', b)])
                    act(RA[:, cols], ps[b][:], AF.Sigmoid, [('ps', b), 'lbT'], [('RA', dr)], bias=bT[:, 0, dr, c:c + 1])
                act(RA[:], RA[:], AF.Exp, [('RA', dr), 'clam'], [('RA', dr)], scale=clam[:, dr, c:c + 1])
                for ti in range(nt):
                    cols = slice(ti * T, (ti + 1) * T)
                    b = cnt % 4
                    cnt += 1
                    mm(ps[b][:], wi[:, dr, c, :], XCB[:, cols], True, True, ['XCB', 'wi'], [('ps', b)])
                    act(IU[:, cols], ps[b][:], AF.Sigmoid, [('ps', b), 'lbT'], [('IU', dr)], bias=bT[:, 1, dr, c:c + 1])
                tt('pool', IU[:], IU[:], XC[:], ALU.mult, [('IU', dr)] + xck, [('IU', dr)])
                act(TMP[:], RA[:], AF.Square, [('RA', dr)], tk)
                act(TMP[:], TMP[:], AF.Sqrt, tk, tk, scale=-1.0, bias=oneb[:, 0:1])
            for dr in range(2):
                RA = RAd[dr]; IU = IUd[dr]
                TMP = HF if dr == 0 else HB
                tk = hfk if dr == 0 else hbk
                tt('dve', IU[:], IU[:], TMP[:], ALU.mult, [('IU', dr)] + tk, [('IU', dr)])
                for si, (s0, L, smp) in enumerate(segs):
                    init = h0[:, dr, c:c + 1] if smp else 0.0
                    if dr == 0:
                        S.op('dve', lambda e, s0=s0, L=L, init=init, RA=RA, IU=IU: e.tensor_tensor_scan(
                            HF[:, s0:s0 + L], RA[:, s0:s0 + L], IU[:, s0:s0 + L], init, ALU.mult, ALU.add),
                            [('RA', dr), ('IU', dr), 'h0'], [('HF', si)])
                    else:
                        S.op('dve', lambda e, s0=s0, L=L, init=init, RA=RA, IU=IU: e.tensor_tensor_scan(
                            V(HB, s0 + L - 1, [[-1, L]]), V(RA, s0 + L - 1, [[-1, L]]), V(IU, s0 + L - 1, [[-1, L]]),
                            init, ALU.mult, ALU.add), [('RA', dr), ('IU', dr), 'h0'], [('HB', si)])
            for si in (1, 2):
                s0, L, _ = segs[si]
                S.op('pool', lambda e, si=si, s0=s0, L=L: e.tensor_copy(stc[:, si - 1, 0:1], HF[:, s0 + L - 1:s0 + L]),
                     [('HF', si)], ['stc'])
                S.op('pool', lambda e, si=si, s0=s0: e.tensor_copy(stc[:, si - 1, 1:2], HB[:, s0:s0 + 1]),
                     [('HB', si)], ['stc'])
                for dr in range(2):
                    dma(D['lru_out'][si - 1, dr, c * 128:(c + 1) * 128].rearrange("(p o) -> p o", o=1),
                        stc[:, si - 1, dr:dr + 1], ['stc'], [('lo', si, dr, c)])
            hk_ = [('HF', 0), ('HF', 1), ('HF', 2), ('HB', 0), ('HB', 1), ('HB', 2)]
            tt('pool', HB[:], HF[:], HB[:], ALU.add, hk_, [('HB', 0), ('HB', 1), ('HB', 2)])
            tt('pool', G[p][:], G[p][:], HB[:], ALU.mult, [('G', p), ('HB', 0), ('HB', 1), ('HB', 2)], [('G', p)])
            dma(D['yin_d'][c * 128:(c + 1) * 128, :], G[p][:], [('G', p)], [('yd', c)])
        S.sb_reset(mark)
        phase_outproj(l, 8, D['lru_wout'], D['yin_d'], x_src, x_dst)

    phase_mod()
    S.sb_reset(base_mark)
    cur = D['xT']
    for l in range(depth_run):
        kind = l % 3
        last = l == depth_run - 1
        if kind == 0:
            phase_attn(l, l // 3, cur, D['xres'])
        elif kind == 1:
            phase_ret(l, cur, D['xres'])
        else:
            phase_lru(l, cur, D['xres'])
        cur = D['xres']
        phase_mlp(l, cur, D['yT'] if last else D['xres'])
    S.emit()
    return nc


def _f32(a):
    return np.ascontiguousarray(np.asarray(a, dtype=np.float32))


RET_PERM = np.concatenate([np.arange(0, 64), np.arange(128, 192), np.arange(64, 128), np.arange(192, 256)])


def _rope_tables_attn():
    p = np.arange(128)
    d = p % 64
    t = np.arange(NS)
    row = (t // 64).astype(np.float32)
    col = (t % 64).astype(np.float32)
    j = np.where(d < 32, d % 16, (d - 32) % 16).astype(np.float32)
    inv = (10000.0 ** (-j / 16.0)).astype(np.float32)
    pos = np.where((d < 32)[:, None], row[None, :], col[None, :]).astype(np.float32)
    ang = (pos * inv[:, None]).astype(np.float32)
    return _f32(np.cos(ang)), _f32(np.sin(ang))


def _rope_tables_ret():
    p = np.arange(128)
    t = np.arange(NS)
    row = (t // 64).astype(np.float32)
    col = (t % 64).astype(np.float32)
    j = (p % 64).astype(np.float32)
    inv = (10000.0 ** (-j / 64.0)).astype(np.float32)
    pos = np.where((p < 64)[:, None], row[None, :], col[None, :]).astype(np.float32)
    ang = (pos * inv[:, None]).astype(np.float32)
    return _f32(np.cos(ang)), _f32(np.sin(ang))


def prep_shared(inp):
    sh = {}
    sh['w_ada'] = _f32(inp['w_ada'])
    sh['b_adaT'] = _f32(np.asarray(inp['b_ada']).reshape(4, 48, 128).transpose(2, 0, 1))
    nm = np.stack([np.asarray(inp['norm_mix']), np.asarray(inp['norm_mlp'])])
    sh['nmT'] = _f32(nm.reshape(2, 4, 8, 128).transpose(3, 0, 1, 2))
    sh['w_up'] = _f32(inp['w_up']); sh['w_down'] = _f32(inp['w_down'])
    W = np.asarray(inp['attn_w_in'])
    q = W[:, :, :1024]; k = W[:, :, 1024:1280]; v = W[:, :, 1280:1536]
    kd = np.concatenate([np.concatenate([k[:, :, g * 64:(g + 1) * 64]] * 2, axis=2) for g in range(4)], axis=2)
    sh['attn_win'] = _f32(np.concatenate([q, kd, v], axis=2))
    sh['attn_wout'] = _f32(inp['attn_w_out'])
    pidx = np.arange(128) % 64
    sh['qgT'] = _f32(np.asarray(inp['attn_q_gain'])[:, pidx].T)
    sh['kgT'] = _f32(np.asarray(inp['attn_k_gain'])[:, pidx].T)
    sh['sinkB'] = _f32(np.broadcast_to(np.asarray(inp['attn_sink'])[None], (128, 2, 16)))
    sh['cosA'], sh['sinA'] = _rope_tables_attn()
    R = np.zeros((128, 128), np.float32)
    for m in range(128):
        if (m % 32) < 16:
            R[m + 16, m] = -1.0
        else:
            R[m - 16, m] = 1.0
    sh['rotA'] = R
    kk = np.arange(128)
    sh['bdones'] = _f32((kk[:, None] // 64) == (kk[None, :] // 64))
    sh['ident'] = _f32(np.eye(128))
    sh['mlo'] = _f32(kk[:, None] >= kk[None, :])
    sh['mhi'] = _f32(kk[:, None] <= kk[None, :])
    sh['mgt'] = _f32(kk[:, None] > kk[None, :])
    Wr = np.asarray(inp['ret_w_in'])[0]
    qc = np.concatenate([Wr[:, hd * 256 + RET_PERM] for hd in range(4)], axis=1)
    kc = np.concatenate([Wr[:, 1024 + hd * 256 + RET_PERM] for hd in range(4)], axis=1)
    sh['ret_win'] = _f32(np.concatenate([qc, kc, Wr[:, 2048:]], axis=1))
    sh['ret_wout'] = _f32(np.asarray(inp['ret_w_out'])[0])
    sh['gainB'] = _f32(np.broadcast_to(np.asarray(inp['ret_gn_gain'])[0][None], (128, 2048)))
    sh['ldB'] = _f32(np.broadcast_to(np.asarray(inp['ret_log_decay'])[0].reshape(8)[None], (128, 8)))
    sh['cosR'], sh['sinR'] = _rope_tables_ret()
    jj = kk[:, None].astype(np.float32); ii = kk[None, :].astype(np.float32)
    sh['MfT'] = _f32(np.maximum(ii - jj, 0)); sh['MbT'] = _f32(np.maximum(jj - ii, 0))
    sh['iq1'] = _f32(np.broadcast_to(ii + 1.0, (128, 128))); sh['iqb'] = _f32(np.broadcast_to(128.0 - ii, (128, 128)))
    sh['cstf'] = _f32(127.0 - jj); sh['cstb'] = _f32(jj)
    sh['lru_win'] = _f32(np.asarray(inp['lru_w_in'])[0]); sh['lru_wout'] = _f32(np.asarray(inp['lru_w_out'])[0])
    cw = np.asarray(inp['lru_conv_w'])[0]; cb = np.asarray(inp['lru_conv_b'])[0]
    cv = np.concatenate([cw, cb[None]], axis=0)
    sh['convT'] = _f32(cv.reshape(5, 8, 128).transpose(2, 0, 1))
    sh['lru_wr'] = _f32(np.asarray(inp['lru_w_r'])[0].transpose(2, 0, 1, 3))
    sh['lru_wi'] = _f32(np.asarray(inp['lru_w_i'])[0].transpose(2, 0, 1, 3))
    b3 = np.stack([np.asarray(inp['lru_b_r'])[0], np.asarray(inp['lru_b_i'])[0], np.asarray(inp['lru_lambda'])[0]])
    sh['lru_bT'] = _f32(b3.reshape(3, 2, 8, 128).transpose(3, 0, 1, 2))
    return sh


def prep_core(inp, b):
    m = {}
    xs = np.asarray(inp['x_sample'])[b]
    xp = np.asarray(inp['x_prompt'])[2 * b:2 * b + 2].reshape(512, 1024)
    m['xT'] = _f32(np.concatenate([xs, xp], axis=0).T)
    cc = np.stack([np.asarray(inp['c'])[b], np.asarray(inp['c_ctx'])], axis=1)
    m['cT'] = _f32(cc.reshape(8, 128, 2).transpose(1, 0, 2))
    ck = np.asarray(inp['cache_attn_k'])[b]
    kt = ck.transpose(0, 3, 2, 1)
    m['kctx'] = _f32(np.concatenate([kt, kt], axis=1))
    cvv = np.asarray(inp['cache_attn_v'])[b]
    m['vctx'] = _f32(cvv.reshape(2, 4, 128, 4, 64).transpose(0, 2, 1, 3, 4))
    sr = np.asarray(inp['state_ret'])[b, 0]
    sr = sr[:, :, RET_PERM, :].reshape(2, 4, 2, 128, 512).transpose(0, 1, 3, 2, 4)
    m['sret'] = _f32(sr)
    sl = np.asarray(inp['state_lru'])[b, 0]
    m['slru'] = _f32(sl.reshape(2, 8, 128).transpose(2, 0, 1))
    return m


_NC_CACHE = {}


def kernel(**inputs):
    depth_run = int(os.environ.get("MK_DEPTH", "4"))
    ncores = int(os.environ.get("MK_CORES", "8"))
    if depth_run not in _NC_CACHE:
        _NC_CACHE[depth_run] = build_nc(depth_run)
    nc = _NC_CACHE[depth_run]
    sh = prep_shared(inputs)
    in_maps = []
    for b in range(ncores):
        m = dict(sh)
        m.update(prep_core(inputs, b))
        in_maps.append(m)
    res = run_bass_kernel_spmd(nc, in_maps, core_ids=list(range(ncores)))
    nb = ncores
    y_prompt = np.zeros((2 * nb, 256, 1024), np.float32)
    y_sample = np.zeros((nb, 4096, 1024), np.float32)
    nk = np.zeros((2 * nb, 2, 256, 4, 64), np.float32)
    nv = np.zeros((2 * nb, 2, 256, 4, 64), np.float32)
    nret = np.zeros((2 * nb, 1, 2, 4, 256, 512), np.float32)
    nlru = np.zeros((2 * nb, 1, 2, 1024), np.float32)
    inv = np.argsort(RET_PERM)
    for b in range(nb):
        r = res.results[b]
        yT = r['yT']
        y_sample[b] = yT[:, :NS].T
        for s in range(2):
            y_prompt[2 * b + s] = yT[:, NS + 256 * s:NS + 256 * (s + 1)].T
        ko = r['k_out'].reshape(2, 4, 64, 2, 256)
        nk[2 * b:2 * b + 2] = ko.transpose(3, 0, 4, 1, 2)
        vo = r['v_out'].reshape(2, 2, 256, 4, 64)
        nv[2 * b:2 * b + 2] = vo.transpose(1, 0, 2, 3, 4)
        ro = r['ret_out'].transpose(0, 1, 2, 4, 3, 5).reshape(2, 2, 4, 256, 512)
        nret[2 * b:2 * b + 2, 0] = ro[:, :, :, inv, :]
        nlru[2 * b:2 * b + 2, 0] = r['lru_out']
    return (y_prompt, y_sample, nk, nv, nret, nlru)
```

```python
import contextlib
import os
import numpy as np
import ml_dtypes
import concourse.bass as bass
import concourse.mybir as mybir
from concourse.bass_utils import run_bass_kernel_spmd

F32 = mybir.dt.float32
BF16 = mybir.dt.bfloat16
AF = mybir.ActivationFunctionType
ALU = mybir.AluOpType

ENGS = ['pe', 'act', 'dve', 'pool', 'sp']
N_DMA_SEMS = 40
SAME_ENG_SYNC = True
SB_BASE = 18560
SB_TOP = 229376

NTOK = 4608
NS = 4096
EPS = 1e-6


class Sched:
    def __init__(self, nc):
        self.nc = nc
        self.ops = {e: [] for e in ENGS}
        self.cnt = {e: 0 for e in ENGS}
        self.known = {e: {} for e in ENGS}
        self.buf = {}
        self.dma_uses = [0] * N_DMA_SEMS
        self.dma_rr = 0
        self.sb_off = SB_BASE
        self.sb_hi = SB_BASE
        self.names = 0
        self.phase = 'init'
        self.scopes = False

    def sb(self, shape, dtype, name=None):
        esz = 2 if dtype == BF16 else 4
        n = 1
        for s in shape[1:]:
            n *= s
        nbytes = (n * esz + 63) // 64 * 64
        self.names += 1
        nm = f"{name or 't'}_{self.names}"
        t = self.nc.alloc_sbuf_tensor_at(nm, list(shape), dtype, offset=self.sb_off)
        self.sb_off += nbytes
        self.sb_hi = max(self.sb_hi, self.sb_off)
        assert self.sb_off <= SB_TOP, f"SBUF overflow {self.sb_off} ({nm})"
        return t

    def sb_mark(self):
        return self.sb_off

    def sb_reset(self, mark):
        self.barrier()
        self.sb_off = mark

    def _deps(self, eng, reads, writes):
        deps = {}

        def add(tok):
            if tok is None:
                return
            k, v = tok
            if deps.get(k, 0) < v:
                deps[k] = v
        for r in reads:
            b = self.buf.get(r)
            if b:
                add(b['w'])
        for w in writes:
            b = self.buf.get(w)
            if b:
                add(b['w'])
                for k, v in b['r'].items():
                    add((k, v))
        out = []
        kn = self.known[eng]
        for k, v in deps.items():
            if k == eng and (eng == 'pe' or not SAME_ENG_SYNC):
                continue
            if kn.get(k, 0) >= v:
                continue
            kn[k] = v
            out.append((k, v))
        return out

    def _mark(self, tok, reads, writes):
        k, v = tok
        for r in reads:
            b = self.buf.setdefault(r, {'w': None, 'r': {}})
            if b['r'].get(k, 0) < v:
                b['r'][k] = v
        for w in writes:
            self.buf[w] = {'w': tok, 'r': {}}

    def op(self, eng, fn, reads=(), writes=()):
        waits = self._deps(eng, reads, writes)
        self.cnt[eng] += 1
        tok = (eng, self.cnt[eng])
        self._mark(tok, reads, writes)
        self.ops[eng].append((fn, waits, True, self.phase))

    def dma(self, fn, reads=(), writes=(), q='sp'):
        s = self.dma_rr
        self.dma_rr = (self.dma_rr + 1) % N_DMA_SEMS
        waits = self._deps(q, reads, writes)
        prev = self.dma_uses[s]
        key = ('dma', s)
        if prev > 0 and self.known[q].get(key, 0) < 16 * prev:
            self.known[q][key] = 16 * prev
            waits.append((key, 16 * prev))
        self.dma_uses[s] += 1
        tok = (key, 16 * self.dma_uses[s])
        self._mark(tok, reads, writes)
        self.ops[q].append((fn, waits, key, self.phase))

    def barrier(self):
        allk = {e: self.cnt[e] for e in ENGS if self.cnt[e] > 0}
        for s in range(N_DMA_SEMS):
            if self.dma_uses[s]:
                allk[('dma', s)] = 16 * self.dma_uses[s]
        for e in ENGS:
            waits = []
            for k, v in allk.items():
                if k == e and e in ('pe', 'sp'):
                    continue
                if self.known[e].get(k, 0) >= v:
                    continue
                self.known[e][k] = v
                waits.append((k, v))
            if waits:
                self.ops[e].append((None, waits, False, self.phase))

    def emit(self):
        nc = self.nc
        self.barrier()
        with contextlib.ExitStack() as st:
            sems = {}
            for e in ENGS:
                sems[e] = st.enter_context(nc.semaphore(f"s_{e}"))
            for s in range(N_DMA_SEMS):
                sems[('dma', s)] = st.enter_context(nc.semaphore(f"s_dma{s}"))
            block = st.enter_context(nc.Block())

            def run(ename, eng):
                cur = [None, None]

                def setscope(lbl):
                    if not self.scopes or lbl == cur[0]:
                        return
                    if cur[1] is not None:
                        cur[1].__exit__(None, None, None)
                    cur[0] = lbl
                    cur[1] = nc.named_scope(lbl) if lbl is not None else None
                    if cur[1] is not None:
                        cur[1].__enter__()
                for fn, waits, inc, ph in self.ops[ename]:
                    setscope(ph)
                    for k, v in waits:
                        eng.wait_ge(sems[k], v)
                    if fn is None:
                        continue
                    ins = fn(eng)
                    if inc is True:
                        ins.then_inc(sems[ename], 1)
                    elif inc is not False:
                        ins.then_inc(sems[inc], 16)
                setscope(None)

            @block.tensor
            def _(e):
                run('pe', e)

            @block.scalar
            def _(e):
                run('act', e)

            @block.vector
            def _(e):
                run('dve', e)

            @block.gpsimd
            def _(e):
                run('pool', e)

            @block.sync
            def _(e):
                run('sp', e)


def fsize(t):
    n = 1
    for s in t.shape[1:]:
        n *= s
    return n


def V(t, off, dims, p0=0, npart=128):
    F = fsize(t)
    return bass.AP(t, p0 * F + off, [[F, npart]] + [list(d) for d in dims])


def build_nc(depth_run=4, scopes=False):
    nc = bass.Bass("TRN2", target_bir_lowering=False)
    S = Sched(nc)
    S.scopes = scopes
    D = {}

    def din(name, shape, dt=F32):
        D[name] = nc.dram_tensor(name, list(shape), dt, kind="ExternalInput").ap()

    def dout(name, shape, dt=F32):
        D[name] = nc.dram_tensor(name, list(shape), dt, kind="ExternalOutput").ap()

    def dscr(name, shape, dt=F32):
        D[name] = nc.dram_tensor(name, list(shape), dt).ap()

    din('xT', [1024, NTOK]); din('cT', [128, 8, 2])
    din('kctx', [2, 128, 4, 512]); din('vctx', [2, 128, 4, 4, 64])
    din('sret', [2, 4, 128, 2, 512]); din('slru', [128, 2, 8])
    din('w_ada', [4, 1024, 6144]); din('b_adaT', [128, 4, 48]); din('nmT', [128, 2, 4, 8])
    din('w_up', [4, 1024, 4096]); din('w_down', [4, 4096, 1024])
    din('attn_win', [2, 1024, 1792]); din('attn_wout', [2, 1024, 1024])
    din('qgT', [128, 2]); din('kgT', [128, 2]); din('sinkB', [128, 2, 16])
    din('cosA', [128, NS]); din('sinA', [128, NS]); din('rotA', [128, 128]); din('bdones', [128, 128])
    din('ident', [128, 128]); din('mlo', [128, 128]); din('mhi', [128, 128]); din('mgt', [128, 128])
    din('ret_win', [1024, 6144]); din('ret_wout', [2048, 1024]); din('gainB', [128, 2048]); din('ldB', [128, 8])
    din('cosR', [128, NS]); din('sinR', [128, NS])
    din('MfT', [128, 128]); din('MbT', [128, 128]); din('iq1', [128, 128]); din('iqb', [128, 128])
    din('cstf', [128, 1]); din('cstb', [128, 1])
    din('lru_win', [1024, 2048]); din('lru_wout', [1024, 1024]); din('convT', [128, 5, 8])
    din('lru_wr', [128, 2, 8, 128]); din('lru_wi', [128, 2, 8, 128]); din('lru_bT', [128, 3, 2, 8])
    dout('yT', [1024, NTOK]); dout('k_out', [2, 256, 512]); dout('v_out', [2, 512, 256])
    dout('ret_out', [2, 2, 4, 128, 2, 512]); dout('lru_out', [2, 2, 1024])
    dscr('xres', [1024, NTOK]); dscr('qT_d', [1024, NTOK], BF16)
    dscr('rq_d', [1024, NTOK], BF16); dscr('rk_d', [1024, NTOK], BF16)
    dscr('ktok_d', [NTOK, 1024], BF16); dscr('vtok_d', [NTOK, 2048], BF16); dscr('sg_d', [NTOK, 2048], BF16)
    dscr('uT_d', [2048, NTOK], BF16)
    dscr('xr_d', [1024, NTOK]); dscr('gate_d', [1024, NTOK], BF16); dscr('yin_d', [1024, NTOK], BF16)

    ps = [nc.alloc_psum_tensor(f"ps{i}", [128, 512], F32) for i in range(8)]
    psbf = ps[7][:].bitcast(BF16)

    def psbf3(k):
        return psbf[:, 0:k * 128].rearrange("p (c n) -> p c n", n=128)

    def pipeline(n, stages, desc=False):
        maxlag = max(lg for lg, _ in stages)
        if desc:
            stages = sorted(stages, key=lambda x: -x[0])
        for t in range(n + maxlag):
            for lg, fn in stages:
                i = t - lg
                if 0 <= i < n:
                    fn(i)

    def act(out, in_, func, reads, writes, bias=None, scale=None):
        kw = {}
        if bias is not None:
            kw['bias'] = bias
        if scale is not None:
            kw['scale'] = scale
        S.op('act', lambda e: e.activation(out=out, in_=in_, func=func, **kw), reads, writes)

    def mm(out, lhsT, rhs, start, stop, reads, writes):
        S.op('pe', lambda e: e.matmul(out, lhsT, rhs, start=start, stop=stop), reads, writes)

    def tt(eng, out, in0, in1, op, reads, writes):
        S.op(eng, lambda e: e.tensor_tensor(out, in0, in1, op), reads, writes)

    def ts(eng, out, in0, s1, s2, op0, op1, reads, writes):
        S.op(eng, lambda e: e.tensor_scalar(out, in0, s1, s2, op0, op1), reads, writes)

    def ts1(eng, out, in0, s1, op0, reads, writes):
        S.op(eng, lambda e: e.tensor_scalar(out, in0, s1, None, op0), reads, writes)

    def stt(eng, out, in0, sc, in1, op0, op1, reads, writes):
        S.op(eng, lambda e: e.scalar_tensor_tensor(out, in0, sc, in1, op0, op1), reads, writes)

    def dma(out, in_, reads, writes, q='sp'):
        S.dma(lambda e: e.dma_start(out=out, in_=in_), reads, writes, q=q)

    castn = [0]

    def cast_load(dst, src, key, nsplit=1):
        castn[0] += 1
        dma(dst, src, [], [key, ('castq', castn[0] % 2)], q='pool')

    def rows(ap2d):
        return ap2d.rearrange("(c p) n -> p c n", p=128)

    ones_bf = S.sb([128, 128], BF16, "ones")
    ident_bf = S.sb([128, 128], BF16, "ident")
    modT = S.sb([128, 4, 48, 2], F32, "modT")
    Gc = S.sb([128, 4, 2, 8, 2], F32, "Gc")
    nmT = S.sb([128, 2, 4, 8], F32, "nmT")
    S.op('pool', lambda e: e.memset(ones_bf[:], 1.0), [], ['ones'])
    epsb = S.sb([128, 1], F32, "epsb"); oneb = S.sb([128, 1], F32, "oneb"); lnb = S.sb([128, 1], F32, "lnb")
    S.op('pool', lambda e: e.memset(epsb[:], EPS), [], ['epsb'])
    S.op('pool', lambda e: e.memset(oneb[:], 1.0), [], ['oneb'])
    S.op('pool', lambda e: e.memset(lnb[:], -2.772588722239781), [], ['lnb'])
    cast_load(ident_bf[:], D['ident'], 'ident')
    dma(nmT[:], D['nmT'], [], ['nmT'])
    base_mark = S.sb_mark()

    def phase_mod():
        S.phase = 'mod'
        cT = S.sb([128, 8, 2], F32, "cT")
        sc = S.sb([128, 8, 2], BF16, "sc")
        bT = S.sb([128, 4, 48], F32, "bT")
        wf = [S.sb([128, 8, 1024], F32, f"waf{i}") for i in range(2)]
        wa = [S.sb([128, 8, 1024], BF16, f"wa{i}") for i in range(2)]
        dma(cT[:], D['cT'], [], ['cT'])
        dma(bT[:], D['b_adaT'], [], ['bT'])
        act(sc[:], cT[:], AF.Silu, ['cT'], ['sc'])
        it = 0
        ceng = ['act', 'dve', 'pool', 'act', 'dve', 'act', 'dve', 'pool']
        for l in range(4):
            pb = ps[l % 2]
            for piece in range(6):
                p2 = it % 2
                it += 1
                src = rows(D['w_ada'][l])[:, :, piece * 1024:(piece + 1) * 1024]
                for k in range(8):
                    dma(wf[p2][:, k, :], src[:, k, :], [], [('waf', p2, k)])
                for k in range(8):
                    if ceng[k] == 'act':
                        act(wa[p2][:, k, :], wf[p2][:, k, :], AF.Copy, [('waf', p2, k)], [('wa', p2, k)])
                    else:
                        S.op(ceng[k], lambda e, p2=p2, k=k: e.tensor_copy(wa[p2][:, k, :], wf[p2][:, k, :]),
                             [('waf', p2, k)], [('wa', p2, k)])
                for n in range(8):
                    j = piece * 8 + n
                    for k in range(8):
                        mm(pb[:, j * 2:j * 2 + 2], wa[p2][:, k, n * 128:(n + 1) * 128], sc[:, k, :],
                           k == 0, k == 7, [('wa', p2, k), 'sc'], [('ps', l % 2)])
            tt('dve', modT[:, l], V(pb, 0, [[2, 48], [1, 2]]), V(bT, l * 48, [[1, 48], [0, 2]]), ALU.add,
               [('ps', l % 2), 'bT'], ['modT'])
            for sub in range(2):
                si = (1 + 3 * sub) * 8
                stt('dve', Gc[:, l, sub], modT[:, l, si:si + 8, :], 1.0,
                    V(nmT, (sub * 4 + l) * 8, [[1, 8], [0, 2]]), ALU.add, ALU.mult,
                    ['modT', 'nmT'], ['Gc'])

    def mcol(l, which, c, sidx):
        return modT[:, l, which * 8 + c, sidx:sidx + 1]

    def gcol(l, sub, c, sidx):
        return Gc[:, l, sub, c, sidx:sidx + 1]

    class NM:
        def __init__(self, T, nbuf_h=2):
            self.T = T
            self.x = [S.sb([128, 8, T], F32, f"x{i}") for i in range(2)]
            self.xsq = S.sb([128, 8, T], BF16, "xsq")
            self.t = S.sb([128, 8, T], F32, "tn")
            self.h = [S.sb([128, 8, T], BF16, f"h{i}") for i in range(nbuf_h)]
            self.rt = S.sb([128, T], F32, "rt")
            self.rstd = S.sb([128, T], F32, "rstd")
            self.nh = nbuf_h

        def load(self, i, src_d):
            T = self.T
            par = i % 2
            v = rows(src_d)[:, :, i * T:(i + 1) * T]
            dma(self.x[par][:, 0:4, :], v[:, 0:4, :], [('xd', i)], [('x', par)])
            dma(self.x[par][:, 4:8, :], v[:, 4:8, :], [('xd', i)], [('x', par)])

        def parts(self, i, l, sub, psb=6, pskey=None):
            T = self.T
            pskey = pskey or ('ps', psb)
            par = i % 2
            hp = i % self.nh
            sidx = 0 if i * T < NS else 1
            x = self.x[par]

            def pa():
                act(self.xsq[:], x[:], AF.Square, [('x', par)], ['xsq'])

            def pb():
                for c in range(8):
                    mm(ps[psb][:, 0:T], ones_bf[:], self.xsq[:, c, :], c == 0, c == 7, ['xsq', 'ones'], [pskey])

            def pc():
                act(self.rt[:], ps[psb][:, 0:T], AF.Ln, [pskey], ['rt'], bias=epsb[:, 0:1], scale=1.0 / 1024)
                act(self.rstd[:], self.rt[:], AF.Exp, ['rt'], ['rstd'], scale=-0.5)

            def pd():
                tt('dve', self.t[:], x[:], V(self.rstd, 0, [[0, 8], [1, T]]), ALU.mult, [('x', par), 'rstd'], ['tn'])

            def pe():
                for c in range(8):
                    act(self.h[hp][:, c, :], self.t[:, c, :], AF.Identity, ['tn', 'Gc', 'modT'], [('h', hp)],
                        bias=mcol(l, 3 * sub, c, sidx), scale=gcol(l, sub, c, sidx))
            return [pa, pb, pc, pd, pe], (self.h[hp], ('h', hp), sidx)

        def run(self, i, l, sub, psb=6, pskey=None):
            fs, res = self.parts(i, l, sub, psb, pskey)
            for f in fs:
                f()
            return res

    def phase_mlp(l, src_d, dst_d):
        S.phase = f'mlp{l}'
        mark = S.sb_mark()
        T = 512
        nt = NTOK // T
        wup = S.sb([128, 8, 4096], BF16, "wup")
        wdn = S.sb([128, 32, 1024], BF16, "wdn")
        for cg in range(8):
            cast_load(wup[:, :, cg * 512:(cg + 1) * 512], rows(D['w_up'][l])[:, :, cg * 512:(cg + 1) * 512], ('wup', cg))
        for cg in range(4):
            for kh in range(2):
                cast_load(wdn[:, kh * 16:(kh + 1) * 16, cg * 256:(cg + 1) * 256],
                          rows(D['w_down'][l])[:, kh * 16:(kh + 1) * 16, cg * 256:(cg + 1) * 256], ('wdn', cg, kh))
        xq = [S.sb([128, T], F32, f"xq{i}") for i in range(3)]
        tc = [S.sb([128, T], F32, f"tc{i}") for i in range(3)]
        xo = [S.sb([128, T], F32, f"xo{i}") for i in range(3)]
        sqc = [S.sb([128, T], BF16, f"sqc{i}") for i in range(2)]
        h = S.sb([128, 8, T], BF16, "hm")
        lnr = S.sb([128, T], F32, "lnr")
        rstd = S.sb([128, T], F32, "rstdm")
        a = S.sb([128, 32, T], BF16, "actb")
        rl = [S.sb([128, T], BF16, f"rl{i}") for i in range(2)]
        akeys = [('a', j) for j in range(32)]
        cn = {'q': 0, 't': 0, 'o': 0}

        def xchunk(i, c):
            return src_d[c * 128:(c + 1) * 128, i * T:(i + 1) * T]

        def na_load(i, c):
            r = (i * 8 + c) % 3
            dma(xq[r][:], xchunk(i, c), [('xd', i)], [('xq', r)])
            act(sqc[c % 2][:], xq[r][:], AF.Square, [('xq', r)], [('sqc', c % 2)])

        def na_mm(i, c):
            mm(ps[6][:], ones_bf[:], sqc[c % 2][:], c == 0, c == 7, [('sqc', c % 2), 'ones'], [('ps', 6)])

        def na_fin(i):
            act(lnr[:], ps[6][:], AF.Ln, [('ps', 6)], ['lnr'], bias=epsb[:, 0:1], scale=1.0 / 1024)
            act(rstd[:], lnr[:], AF.Exp, ['lnr'], ['rstdm'], scale=-0.5)

        def norm_a(i):
            for c in range(8):
                na_load(i, c)
                na_mm(i, c)
            na_fin(i)

        def norm_a_spread(i, n):
            c2 = n - 10
            if 0 <= c2 < 8:
                na_mm(i, c2)
            c1 = n - 8
            if 0 <= c1 < 8:
                na_load(i, c1)
            if n == 19:
                na_fin(i)

        def norm_b(i):
            sidx = 0 if i * T < NS else 1
            for c in range(8):
                r = cn['t'] % 3
                cn['t'] += 1
                dma(tc[r][:], xchunk(i, c), [('xd', i)], [('tc', r)])
                tt('dve', tc[r][:], tc[r][:], rstd[:], ALU.mult, [('tc', r), 'rstdm'], [('tc', r)])
                act(h[:, c, :], tc[r][:], AF.Identity, [('tc', r), 'Gc', 'modT'], ['hm'],
                    bias=mcol(l, 3, c, sidx), scale=gcol(l, 1, c, sidx))

        norm_a(0)
        norm_b(0)
        for i in range(nt):
            sidx = 0 if i * T < NS else 1
            for n in range(32):
                b = n % 3
                for k in range(8):
                    mm(ps[b][:], wup[:, k, n * 128:(n + 1) * 128], h[:, k, :], k == 0, k == 7, ['hm', ('wup', n // 4)], [('ps', b)])
                act(rl[n % 2][:], ps[b][:], AF.Relu, [('ps', b)], [('rl', n % 2)])
                tt('pool', a[:, n, :], rl[n % 2][:], rl[n % 2][:], ALU.mult, [('rl', n % 2)], [('a', n)])
                if i + 1 < nt:
                    norm_a_spread(i + 1, n)
            for m in range(8):
                b = 3 + m % 2
                r = cn['o'] % 3
                cn['o'] += 1
                dma(xo[r][:], xchunk(i, m), [('xd', i)], [('xo', r)])
                for k in range(32):
                    mm(ps[b][:], wdn[:, k, m * 128:(m + 1) * 128], a[:, k, :], k == 0, k == 31,
                       (akeys if k == 0 else []) + [('wdn', m // 2, k // 16)], [('ps', b)])
                stt('dve', xo[r][:], ps[b][:], mcol(l, 5, m, sidx), xo[r][:], ALU.mult, ALU.add,
                    [('ps', b), ('xo', r), 'modT'], [('xo', r)])
                dma(dst_d[m * 128:(m + 1) * 128, i * T:(i + 1) * T], xo[r][:], [('xo', r)], [('xdo', i, m)])
                if m == 1 and i + 1 < nt:
                    norm_b(i + 1)
        S.sb_reset(mark)

    def phase_outproj(l, nk, w_dram, src_feat_d, x_src, x_dst):
        S.phase = f'outproj{l}'
        mark = S.sb_mark()
        T = 512
        w = S.sb([128, nk, 1024], BF16, "wout")
        for k4 in range(nk // 4):
            cast_load(w[:, k4 * 4:(k4 + 1) * 4, :], rows(w_dram)[:, k4 * 4:(k4 + 1) * 4, :], 'wout')
        u = [S.sb([128, nk, T], BF16, f"u{i}") for i in range(2)]
        x = [S.sb([128, 8, T], F32, f"xr{i}") for i in range(2)]
        xo = [S.sb([128, 8, T], F32, f"xo{i}") for i in range(2)]
        nt = NTOK // T

        def load(i):
            par = i % 2
            dma(u[par][:], rows(src_feat_d)[:, :, i * T:(i + 1) * T], [], [('u', par)])
            dma(x[par][:], rows(x_src)[:, :, i * T:(i + 1) * T], [('xd', i)], [('x', par)])
        load(0)
        load(1)
        for i in range(nt):
            par = i % 2
            sidx = 0 if i * T < NS else 1
            for m in range(8):
                b = m % 4
                for k in range(nk):
                    mm(ps[b][:], w[:, k, m * 128:(m + 1) * 128], u[par][:, k, :], k == 0, k == nk - 1,
                       [('u', par), 'wout'], [('ps', b)])
                stt('dve', xo[par][:, m, :], ps[b][:], mcol(l, 2, m, sidx), x[par][:, m, :], ALU.mult, ALU.add,
                    [('ps', b), ('x', par), 'modT'], [('xo', par)])
            dma(rows(x_dst)[:, :, i * T:(i + 1) * T], xo[par][:], [('xo', par)], [('xd', i)])
            if i + 2 < nt:
                load(i + 2)
        S.sb_reset(mark)

    def phase_attn(l, slot, x_src, x_dst):
        S.phase = f'A1_{l}'
        mark0 = S.sb_mark()
        vext = S.sb([128, 36, 4, 66], BF16, "vext")
        kdup = S.sb([128, 4, NTOK], BF16, "kdup")
        S.op('pool', lambda e: e.memset(vext[:, :, :, 64:66], 1.0), [], ['vext1'])
        mark1 = S.sb_mark()
        T = 256
        w = S.sb([128, 8, 1792], BF16, "win")
        for k in range(8):
            cast_load(w[:, k, :], rows(D['attn_win'][slot])[:, k, :], 'win')
        cosA = S.sb([128, NS], F32, "cosA"); sinA = S.sb([128, NS], F32, "sinA")
        dma(cosA[:], D['cosA'], [], ['cosA']); dma(sinA[:], D['sinA'], [], ['sinA'])
        rot = S.sb([128, 128], BF16, "rot"); bdo = S.sb([128, 128], BF16, "bdo")
        cast_load(rot[:], D['rotA'], 'rot'); cast_load(bdo[:], D['bdones'], 'bdo')
        qg = S.sb([128, 2], F32, "qg"); kg = S.sb([128, 2], F32, "kg")
        dma(qg[:], D['qgT'], [], ['qg']); dma(kg[:], D['kgT'], [], ['kg'])
        nm = NM(T)
        qt = [S.sb([128, 8, T], BF16, f"qt{i}") for i in range(2)]
        sqb = [S.sb([128, 2, T], BF16, f"sqb{i}") for i in range(3)]
        rtb = [S.sb([128, 2, T], F32, f"rtb{i}") for i in range(3)]
        rsb = [S.sb([128, 2, T], F32, f"rsb{i}") for i in range(3)]
        qnb = [S.sb([128, 2, T], BF16, f"qnb{i}") for i in range(3)]
        t1b = [S.sb([128, 2, T], F32, f"t1b{i}") for i in range(3)]
        t2b = [S.sb([128, 2, T], F32, f"t2b{i}") for i in range(2)]
        kf = S.sb([128, 4, T], F32, "kf")
        vf = [S.sb([128, 256], F32, f"vf{i}") for i in range(2)]
        nt = NTOK // T
        nm.load(0, x_src); nm.load(1, x_src)
        hs = {}
        hs[0] = nm.run(0, l, 0, psb=7, pskey=('ps', 7))
        pend = [None]
        NIT = nt * 6

        def info(it):
            i, p = divmod(it, 6)
            t0 = i * T
            sample = t0 < NS
            return i, p, t0, sample, i % 2

        def pv3(b):
            return V(ps[b], 0, [[T, 2], [1, T]])

        def s0(it):
            i, p, t0, sample, par = info(it)
            if i + 1 < nt:
                if p == 0:
                    pend[0], hs[i + 1] = nm.parts(i + 1, l, 0, psb=7, pskey=('ps', 7))
                if p < 5:
                    pend[0][p]()
            h, hk, sidx = hs[i]
            b = it % 4
            for half in range(2):
                col0 = (2 * p + half) * 128
                for k in range(8):
                    mm(ps[b][:, half * T:(half + 1) * T], w[:, k, col0:col0 + 128], h[:, k, :], k == 0, k == 7,
                       [hk, 'win'], [('ps', b)])
            act(sqb[it % 3][:], pv3(b), AF.Square, [('ps', b)], [('sqb', it % 3)])

        def s2(it):
            i, p, t0, sample, par = info(it)
            nb_ = 4 + it % 3
            r3 = it % 3
            for half in range(2):
                mm(ps[nb_][:, half * T:(half + 1) * T], bdo[:], sqb[r3][:, half, :], True, True,
                   [('sqb', r3), 'bdo'], [('ps', nb_)])
            act(rtb[r3][:], pv3(nb_), AF.Ln, [('ps', nb_)], [('rtb', r3)], bias=epsb[:, 0:1], scale=1.0 / 64)
            act(rsb[r3][:], rtb[r3][:], AF.Exp, [('rtb', r3)], [('rsb', r3)], scale=-0.5)
            b = it % 4
            isq = p < 4
            gcolp = (qg if isq else kg)[:, slot:slot + 1]
            if sample:
                stt('dve', qnb[r3][:], pv3(b), gcolp, rsb[r3][:], ALU.mult, ALU.mult,
                    [('ps', b), ('rsb', r3), 'qg', 'kg'], [('qnb', r3)])
            else:
                if isq:
                    dst = qt[par][:, 2 * p:2 * p + 2, :]; dkey = ('qt', par)
                else:
                    g0 = 2 * (p - 4)
                    dst = kdup[:, g0:g0 + 2, t0:t0 + T]; dkey = ('kdup', i)
                stt('dve', dst, pv3(b), gcolp, rsb[r3][:], ALU.mult, ALU.mult,
                    [('ps', b), ('rsb', r3), 'qg', 'kg'], [dkey])
                if not isq:
                    stt('dve', kf[:, g0:g0 + 2, :], pv3(b), gcolp, rsb[r3][:], ALU.mult, ALU.mult,
                        [('ps', b), ('rsb', r3), 'qg', 'kg'], ['kf'])
                finish(it)

        def s4(it):
            i, p, t0, sample, par = info(it)
            if not sample:
                return
            nb_ = 4 + it % 3
            r3 = it % 3
            for half in range(2):
                mm(ps[nb_][:, half * T:(half + 1) * T], rot[:], qnb[r3][:, half, :], True, True,
                   [('qnb', r3), 'rot'], [('ps', nb_)])
            tt('pool', t1b[r3][:], qnb[r3][:], V(cosA, t0, [[0, 2], [1, T]]), ALU.mult, [('qnb', r3), 'cosA'], [('t1b', r3)])

        def s5(it):
            i, p, t0, sample, par = info(it)
            if not sample:
                return
            nb_ = 4 + it % 3
            r3 = it % 3
            tt('dve', t2b[it % 2][:], pv3(nb_), V(sinA, t0, [[0, 2], [1, T]]), ALU.mult, [('ps', nb_), 'sinA'], [('t2b', it % 2)])
            if p < 4:
                dst = qt[par][:, 2 * p:2 * p + 2, :]; dkey = ('qt', par)
            else:
                g0 = 2 * (p - 4)
                dst = kdup[:, g0:g0 + 2, t0:t0 + T]; dkey = ('kdup', i)
            tt('pool', dst, t1b[r3][:], t2b[it % 2][:], ALU.add, [('t1b', r3), ('t2b', it % 2)], [dkey])
            finish(it)

        def finish(it):
            i, p, t0, sample, par = info(it)
            if p == 3:
                dma(rows(D['qT_d'])[:, :, t0:t0 + T], qt[par][:], [('qt', par)], [('qd', i)])
            if p == 5 and not sample:
                pt0 = t0 - NS
                for g in range(4):
                    dma(D['k_out'][slot, g * 64:(g + 1) * 64, pt0:pt0 + T], kf[0:64, g, :], ['kf'], [('ko', i, g)])

        def sv(it):
            i, p, t0, sample, par = info(it)
            if p != 5:
                return
            h, hk, sidx = hs[i]
            for blk in range(T // 128):
                gb = (t0 // 128) + blk
                for k in range(8):
                    mm(ps[7][:, 256:512], h[:, k, blk * 128:(blk + 1) * 128], w[:, k, 1536:1792], k == 0, k == 7,
                       [hk, 'win'], [('ps', 7)])
                act(vext[:, gb, :, 0:64], V(ps[7], 256, [[64, 4], [1, 64]]), AF.Copy, [('ps', 7)], [('vext', gb)])
                if not sample:
                    act(vf[blk][:], ps[7][:, 256:512], AF.Copy, [('ps', 7)], [('vf', blk)])
                    r0 = t0 - NS + blk * 128
                    dma(D['v_out'][slot, r0:r0 + 128, :], vf[blk][:], [('vf', blk)], [('vo', gb)])
            if i + 2 < nt:
                nm.load(i + 2, x_src)

        pipeline(NIT, [(0, s0), (0, sv), (1, s2), (2, s4), (3, s5)])
        S.sb_reset(mark1)
        S.phase = f'A2_{l}'
        wo = S.sb([128, 8, 1024], BF16, "wo")
        for k4 in range(2):
            cast_load(wo[:, k4 * 4:(k4 + 1) * 4, :], rows(D['attn_wout'][slot])[:, k4 * 4:(k4 + 1) * 4, :], 'wo')
        kz1 = S.sb([128, 4, NTOK], BF16, "kz1")
        S.op('pool', lambda e: e.memset(kz1[0:64, :, :], 0.0), [], ['kz1a'])
        act(kz1[64:128, :, :], kdup[64:128, :, :], AF.Copy, ['kdupall'], ['kz1b'])
        S.op('pool', lambda e: e.memset(kdup[64:128, :, :], 0.0), [], ['kdupall', 'kz0'])
        kz = [kdup, kz1]
        kzk = [['kz0'], ['kz1a', 'kz1b']]
        kcz = [S.sb([128, 4, 512], BF16, f"kcz{i}") for i in range(2)]
        S.op('pool', lambda e: e.memset(kcz[0][64:128, :, :], 0.0), [], ['kcz0z'])
        S.op('pool', lambda e: e.memset(kcz[1][0:64, :, :], 0.0), [], ['kcz1z'])
        cast_load(kcz[0][0:64, :, :], D['kctx'][slot][0:64], 'kcz0')
        cast_load(kcz[1][64:128, :, :], D['kctx'][slot][64:128], 'kcz1')
        kczk = [['kcz0', 'kcz0z'], ['kcz1', 'kcz1z']]
        vctx = S.sb([128, 4, 4, 66], BF16, "vctx")
        S.op('pool', lambda e: e.memset(vctx[:, :, :, 64:66], 1.0), [], ['vctx1'])
        for j in range(4):
            cast_load(vctx[:, j, :, 0:64], D['vctx'][slot][:, j], ('vctx', j))
        mlo = S.sb([128, 128], BF16, "mlo"); mhi = S.sb([128, 128], BF16, "mhi")
        cast_load(mlo[:], D['mlo'], 'mlo'); cast_load(mhi[:], D['mhi'], 'mhi')
        sk = S.sb([128, 16], F32, "sk"); esk = S.sb([128, 16], F32, "esk")
        dma(sk[:], D['sinkB'][:, slot, :], [], ['sk'])
        act(esk[:], sk[:], AF.Exp, ['sk'], ['esk'])
        TQ = 512
        qs = [S.sb([128, 8, TQ], BF16, f"qs{i}") for i in range(2)]
        xs = [S.sb([128, 8, TQ], F32, f"xs{i}") for i in range(2)]
        oT = [S.sb([128, 8, TQ], BF16, f"oT{i}") for i in range(2)]
        osb = [S.sb([128, 1024], BF16, f"osb{i}") for i in range(2)]
        NPT = 5
        PT = [S.sb([128, 7, 128], BF16, f"PT{i}") for i in range(NPT)]
        den = [S.sb([128, 4], F32, f"den{i}") for i in range(2)]
        rec = [S.sb([128, 4], F32, f"rec{i}") for i in range(2)]
        ng = NTOK // TQ

        def load2(gi):
            par = gi % 2
            dma(qs[par][:], rows(D['qT_d'])[:, :, gi * TQ:(gi + 1) * TQ], [('qd', 2 * gi), ('qd', 2 * gi + 1)], [('qs', par)])
            dma(xs[par][:], rows(x_src)[:, :, gi * TQ:(gi + 1) * TQ], [('xd', gi)], [('xs', par)])
        load2(0); load2(1)
        NITEM = 36 * 16
        ring = [0]
        plan = {}

        def keys_of(n, hh):
            g = hh // 4
            hp = hh % 2
            keys = []
            if n < 32:
                for j in range(4):
                    keys.append((kcz[hp][:, g, j * 128:(j + 1) * 128], vctx[:, j, g, 0:65], None,
                                 kczk[hp], [('vctx', j), 'vctx1']))
                for nn_, mk in ((n - 1, mlo), (n, None), (n + 1, mhi)):
                    if 0 <= nn_ < 32:
                        keys.append((kz[hp][:, g, nn_ * 128:(nn_ + 1) * 128], vext[:, nn_, g, 0:65], mk,
                                     kzk[hp], [('vext', nn_), 'vext1']))
            else:
                sq0 = 32 + ((n - 32) // 2) * 2
                for nn_ in (sq0, sq0 + 1):
                    keys.append((kz[hp][:, g, nn_ * 128:(nn_ + 1) * 128], vext[:, nn_, g, 0:65], None,
                                 kzk[hp], [('vext', nn_), 'vext1']))
            return keys

        def a0(it):
            n, hh = divmod(it, 16)
            gi, bq = divmod(n, 4)
            gpar = gi % 2
            c = hh // 2
            pr0 = (hh % 2) * 64
            keys = keys_of(n, hh)
            nk = len(keys)
            nbank = (nk + 3) // 4
            banks = [(ring[0] + j) % 5 for j in range(nbank)]
            ring[0] = (ring[0] + nbank) % 5
            plan[it] = (keys, banks)
            pt = PT[it % NPT]
            ptk = ('PT', it % NPT)
            for idx, (kT, vE, mk, kk, vk) in enumerate(keys):
                bk = banks[idx // 4]
                mm(ps[bk][:, (idx % 4) * 128:(idx % 4 + 1) * 128], kT,
                   qs[gpar][:, c, bq * 128:(bq + 1) * 128], True, True,
                   kk + [('qs', gpar)], [('ps', bk)])
            for j, bk in enumerate(banks):
                n0 = min(nk - 4 * j, 4)
                act(pt[:, 4 * j:4 * j + n0, :], V(ps[bk], 0, [[128, n0], [1, 128]]), AF.Exp, [('ps', bk)], [ptk], scale=0.125)
            for idx, (kT, vE, mk, kk, vk) in enumerate(keys):
                if mk is not None:
                    tt('pool', pt[:, idx, :], pt[:, idx, :], mk[:], ALU.mult, [ptk, 'mlo', 'mhi'], [ptk])

        def a1(it):
            n, hh = divmod(it, 16)
            gi, bq = divmod(n, 4)
            g = hh // 4
            keys, banks = plan.pop(it)
            nk = len(keys)
            pt = PT[it % NPT]
            ptk = ('PT', it % NPT)
            opar = n % 2
            q4 = it // 4
            ob = 5 + q4 % 2
            oc = (hh % 4) * 128
            for idx, (kT, vE, mk, kk, vk) in enumerate(keys):
                mm(ps[ob][:, oc:oc + 65], pt[:, idx, :], vE, idx == 0, idx == nk - 1, [ptk] + vk, [('ps', ob)])
            if hh % 4 == 3:
                d_ = den[q4 % 2]; r_ = rec[q4 % 2]
                dk = ('den', q4 % 2); rk = ('rec', q4 % 2)
                tt('dve', d_[:], V(ps[ob], 64, [[128, 4]]), esk[:, 4 * g:4 * g + 4], ALU.add, [('ps', ob), 'esk'], [dk])
                S.op('dve', lambda e, d_=d_, r_=r_: e.reciprocal(r_[:], d_[:]), [dk], [rk])
                tt('dve', V(osb[opar], g * 256, [[64, 4], [1, 64]]), V(ps[ob], 0, [[128, 4], [1, 64]]),
                   V(r_, 0, [[1, 4], [0, 64]]), ALU.mult, [('ps', ob), rk], [('osb', opar)])

        def a2(it):
            n, hh = divmod(it, 16)
            if hh != 15:
                return
            gi, bq = divmod(n, 4)
            gpar = gi % 2
            opar = n % 2
            sidx = 0 if gi * TQ < NS else 1
            for c in range(8):
                S.op('pe', lambda e, c=c, opar=opar: e.transpose(psbf[:, c * 128:(c + 1) * 128],
                                                                  osb[opar][:, c * 128:(c + 1) * 128], ident_bf[:]),
                     [('osb', opar), 'ident'], [('ps', 7)])
            S.op('dve', lambda e, gpar=gpar, bq=bq: e.tensor_copy(oT[gpar][:, :, bq * 128:(bq + 1) * 128], psbf3(8)), [('ps', 7)], [('oT', gpar)])
            if bq == 3:
                for m in range(8):
                    for k in range(8):
                        mm(ps[7][:], wo[:, k, m * 128:(m + 1) * 128], oT[gpar][:, k, :], k == 0, k == 7,
                           [('oT', gpar), 'wo'], [('ps', 7)])
                    stt('dve', xs[gpar][:, m, :], ps[7][:], mcol(l, 2, m, sidx), xs[gpar][:, m, :], ALU.mult, ALU.add,
                        [('ps', 7), ('xs', gpar), 'modT'], [('xs', gpar)])
                dma(rows(x_dst)[:, :, gi * TQ:(gi + 1) * TQ], xs[gpar][:], [('xs', gpar)], [('xd', gi)])
                if gi + 2 < ng:
                    load2(gi + 2)

        pipeline(NITEM, [(0, a0), (3, a1), (5, a2)])
        S.sb_reset(mark0)

    def phase_ret(l, x_src, x_dst):
        T = 256
        nt = NTOK // T
        S.phase = 'R1a'
        mark = S.sb_mark()
        T = 512
        nt = NTOK // T
        w = S.sb([128, 8, 2048], BF16, "rwin")
        for k in range(8):
            cast_load(w[:, k, :], rows(D['ret_win'])[:, k, 0:2048], 'rwin')
        cosR = S.sb([128, NS], F32, "cosR"); sinR = S.sb([128, NS], F32, "sinR")
        dma(cosR[:], D['cosR'], [], ['cosR']); dma(sinR[:], D['sinR'], [], ['sinR'])
        nm = NM(T)
        qk = [S.sb([128, 16, T], BF16, "qk0")] * 2
        ktk = [S.sb([128, T // 128, 1024], BF16, "ktk0")] * 2
        tm = [[S.sb([128, T], F32, f"tm{a}{i}") for i in range(2)] for a in range(4)]
        nm.load(0, x_src); nm.load(1, x_src)
        cnt = 0
        hs = {0: nm.run(0, l, 0, psb=6)}
        for i in range(nt):
            par = 0
            t0 = i * T
            h, hk, sidx = hs.pop(i)
            sample = sidx == 0
            for pi in range(8):
                if pi == 4 and i + 1 < nt:
                    hs[i + 1] = nm.run(i + 1, l, 0, psb=6)
                cA = pi * 2
                u = cnt % 2
                cnt += 1
                pA = ps[u]; pB = ps[2 + u]
                for half, pp in ((0, pA), (1, pB)):
                    col0 = (cA + half) * 128
                    for k in range(8):
                        mm(pp[:, 0:T], w[:, k, col0:col0 + 128], h[:, k, :], k == 0, k == 7, [hk, 'rwin'],
                           [('ps', u + 2 * half)])
                dA = qk[par][:, cA, :]; dB = qk[par][:, cA + 1, :]
                if sample:
                    cs = cosR[:, t0:t0 + T]; sn = sinR[:, t0:t0 + T]
                    tt('dve', tm[0][u][:], pA[:, 0:T], cs, ALU.mult, [('ps', u), 'cosR'], [('tm0', u)])
                    tt('dve', tm[1][u][:], pB[:, 0:T], sn, ALU.mult, [('ps', 2 + u), 'sinR'], [('tm1', u)])
                    tt('dve', tm[2][u][:], pB[:, 0:T], cs, ALU.mult, [('ps', 2 + u), 'cosR'], [('tm2', u)])
                    tt('dve', tm[3][u][:], pA[:, 0:T], sn, ALU.mult, [('ps', u), 'sinR'], [('tm3', u)])
                    tt('pool', dA, tm[0][u][:], tm[1][u][:], ALU.subtract, [('tm0', u), ('tm1', u)], [('qk', par)])
                    tt('pool', dB, tm[2][u][:], tm[3][u][:], ALU.add, [('tm2', u), ('tm3', u)], [('qk', par)])
                else:
                    act(dA, pA[:, 0:T], AF.Copy, [('ps', u)], [('qk', par)])
                    act(dB, pB[:, 0:T], AF.Copy, [('ps', 2 + u)], [('qk', par)])
            for blk in range(T // 128):
                for c in range(8):
                    S.op('pe', lambda e, c=c, blk=blk, par=par: e.transpose(
                        psbf[:, c * 128:(c + 1) * 128], qk[par][:, 8 + c, blk * 128:(blk + 1) * 128], ident_bf[:]),
                        [('qk', par), 'ident'], [('ps', 7)])
                act(ktk[par][:, blk, :], psbf, AF.Copy, [('ps', 7)], [('ktk', par)])
            dma(rows(D['rq_d'])[:, :, t0:t0 + T], qk[par][:, 0:8, :], [('qk', par)], [('rqd', i)])
            dma(rows(D['rk_d'])[:, :, t0:t0 + T], qk[par][:, 8:16, :], [('qk', par)], [('rkd', i)])
            dma(D['ktok_d'][t0:t0 + T, :].rearrange("(b p) n -> p b n", p=128), ktk[par][:], [('ktk', par)], [('ktd', i)])
            if i + 2 < nt:
                nm.load(i + 2, x_src)
        S.sb_reset(mark)
        S.phase = 'R1b'
        T = 256
        nt = NTOK // T
        w2 = S.sb([128, 8, 4096], BF16, "rwin2")
        for k in range(8):
            for hh in range(2):
                cast_load(w2[:, k, hh * 2048:(hh + 1) * 2048], rows(D['ret_win'])[:, k, 2048 + hh * 2048:2048 + (hh + 1) * 2048], 'rwin2')
        gB = S.sb([128, 2048], F32, "gB")
        dma(gB[:], D['gainB'], [], ['gB'])
        nm = NM(T)
        vt = [S.sb([128, 2, 2048], BF16, f"vt{i}") for i in range(2)]
        sgt = [S.sb([128, 2, 2048], BF16, f"sgt{i}") for i in range(2)]
        sgf = [S.sb([128, 512], F32, f"sgf{i}") for i in range(2)]
        nm.load(0, x_src); nm.load(1, x_src)
        cnt = 0
        hs = {0: nm.run(0, l, 0, psb=6)}
        for i in range(nt):
            par = i % 2
            t0 = i * T
            h, hk, sidx = hs.pop(i)
            for blk in range(2):
                if blk == 1 and i + 1 < nt:
                    hs[i + 1] = nm.run(i + 1, l, 0, psb=6)
                for grp in range(8):
                    b = cnt % 4
                    cnt += 1
                    for k in range(8):
                        mm(ps[b][:], h[:, k, blk * 128:(blk + 1) * 128], w2[:, k, grp * 512:(grp + 1) * 512],
                           k == 0, k == 7, [hk, 'rwin2'], [('ps', b)])
                    if grp < 4:
                        S.op('dve', lambda e, par=par, blk=blk, grp=grp, b=b: e.tensor_copy(vt[par][:, blk, grp * 512:(grp + 1) * 512], ps[b][:]),
                             [('ps', b)], [('vt', par)])
                    else:
                        u = cnt % 2
                        act(sgf[u][:], ps[b][:], AF.Silu, [('ps', b)], [('sgf', u)])
                        tt('pool', sgt[par][:, blk, (grp - 4) * 512:(grp - 3) * 512], sgf[u][:],
                           gB[:, (grp - 4) * 512:(grp - 3) * 512], ALU.mult, [('sgf', u), 'gB'], [('sgt', par)])
            dma(D['vtok_d'][t0:t0 + T, :].rearrange("(b p) n -> p b n", p=128), vt[par][:], [('vt', par)], [('vtd', i)])
            dma(D['sg_d'][t0:t0 + T, :].rearrange("(b p) n -> p b n", p=128), sgt[par][:], [('sgt', par)], [('sgd', i)])
            if i + 2 < nt:
                nm.load(i + 2, x_src)
        S.sb_reset(mark)
        S.phase = 'R2'
        ld = S.sb([128, 8], F32, "ld")
        dma(ld[:], D['ldB'], [], ['ld'])
        gC = S.sb([128, 8], F32, "gC")
        act(gC[:], ld[:], AF.Exp, ['ld'], ['gC'], scale=128.0)
        Dc = S.sb([128, 4, 128], F32, "Dc")
        wq = S.sb([128, 2, 4, 128], F32, "wq")
        wst = S.sb([128, 2, 4], F32, "wst")
        cmark = S.sb_mark()
        cm = {}
        for nme in ('MfT', 'MbT', 'iq1', 'iqb', 'mhi', 'mgt'):
            cm[nme] = S.sb([128, 128], F32, nme)
            dma(cm[nme][:], D[nme], [], [nme])
        cst = S.sb([128, 2], F32, "cst")
        dma(cst[:, 0:1], D['cstf'], [], ['cst']); dma(cst[:, 1:2], D['cstb'], [], ['cst'])
        e1 = S.sb([128, 128], F32, "e1"); e2 = S.sb([128, 128], F32, "e2")
        for hd in range(4):
            lf = ld[:, hd:hd + 1]; lb = ld[:, 4 + hd:5 + hd]
            act(e1[:], cm['MfT'][:], AF.Exp, ['MfT', 'ld'], ['e1'], scale=lf)
            tt('dve', e1[:], e1[:], cm['mhi'][:], ALU.mult, ['e1', 'mhi'], ['e1'])
            act(e2[:], cm['MbT'][:], AF.Exp, ['MbT', 'ld'], ['e2'], scale=lb)
            tt('dve', e2[:], e2[:], cm['mgt'][:], ALU.mult, ['e2', 'mgt'], ['e2'])
            tt('dve', e1[:], e1[:], e2[:], ALU.add, ['e1', 'e2'], ['e1'])
            ts1('dve', Dc[:, hd, :], e1[:], 0.0625, ALU.mult, ['e1'], ['Dc'])
            act(wq[:, 0, hd, :], cm['iq1'][:], AF.Exp, ['iq1', 'ld'], ['wq'], scale=lf)
            act(wq[:, 1, hd, :], cm['iqb'][:], AF.Exp, ['iqb', 'ld'], ['wq'], scale=lb)
            act(wst[:, 0, hd:hd + 1], cst[:, 0:1], AF.Exp, ['cst', 'ld'], ['wst'], scale=lf, bias=lnb[:, 0:1])
            act(wst[:, 1, hd:hd + 1], cst[:, 1:2], AF.Exp, ['cst', 'ld'], ['wst'], scale=lb, bias=lnb[:, 0:1])
        S.sb_reset(cmark)
        qTh = S.sb([128, 2, NS], BF16, "qTh"); kTh = S.sb([128, 2, NS], BF16, "kTh")
        kth = S.sb([128, 32, 256], BF16, "kth"); vth = S.sb([128, 32, 512], BF16, "vth")
        Sst = S.sb([128, 32, 2, 512], BF16, "Sst")
        Sf = [S.sb([128, 2, 512], F32, f"Sf{i}") for i in range(2)]
        Sb = [S.sb([128, 2, 512], F32, f"Sb{i}") for i in range(2)]
        Sfb = [S.sb([128, 2, 512], BF16, f"Sfb{i}") for i in range(4)]
        kb = [S.sb([128, 256], BF16, f"kb{i}") for i in range(2)]
        kf2 = [S.sb([128, 256], BF16, f"kf{i}") for i in range(2)]
        PTr = [S.sb([128, 128], BF16, f"PTr{i}") for i in range(3)]
        NQF = 5
        qf = [S.sb([128, 2, 2, 128], BF16, f"qf{i}") for i in range(NQF)]
        st6 = [S.sb([128, 6], F32, f"st6{i}") for i in range(2)]
        mv = [S.sb([128, 2], F32, f"mv{i}") for i in range(4)]
        lnv = [S.sb([128, 1], F32, f"lnv{i}") for i in range(2)]
        rsd = [S.sb([128, 1], F32, f"rsd{i}") for i in range(3)]
        NOS = 5
        Osb = [S.sb([128, 512], F32, f"Osb{i}") for i in range(NOS)]
        NSG = 5
        sgc = [S.sb([128, 512], BF16, f"sgc{i}") for i in range(NSG)]
        uu = [S.sb([128, 512], BF16, f"uu{i}") for i in range(2)]
        uT = [S.sb([128, 4, 128], BF16, f"uT{i}") for i in range(2)]
        seqs = [(0, 32, True, -1), (32, 2, False, 0), (34, 2, False, 1)]
        for (b0, nb, smp, sq_i) in seqs:
            L = nb * 128
            c0 = b0 * 128
            for hd in range(4):
                dma(kth[:, 0:nb, :], D['ktok_d'][c0:c0 + L, hd * 256:(hd + 1) * 256].rearrange("(b p) n -> p b n", p=128), [], ['kth'])
                dma(vth[:, 0:nb, :], D['vtok_d'][c0:c0 + L, hd * 512:(hd + 1) * 512].rearrange("(b p) n -> p b n", p=128), [], ['vth'])
                dma(qTh[:, :, 0:L], rows(D['rq_d'])[:, 2 * hd:2 * hd + 2, c0:c0 + L], [], ['qTh'])
                dma(kTh[:, :, 0:L], rows(D['rk_d'])[:, 2 * hd:2 * hd + 2, c0:c0 + L], [], ['kTh'])
                if smp:
                    dma(Sf[0][:], D['sret'][0, hd], [], [('Sf', 0)])
                    dma(Sb[0][:], D['sret'][1, hd], [], [('Sb', 0)])
                else:
                    S.op('pool', lambda e: e.memset(Sf[0][:], 0.0), [], [('Sf', 0)])
                    S.op('pool', lambda e: e.memset(Sb[0][:], 0.0), [], [('Sb', 0)])
                act(Sst[:, nb - 1], Sb[0][:], AF.Copy, [('Sb', 0)], [('Sst', nb - 1)])
                lo = 1 if smp else 0
                nbw = nb - lo

                def B0(j):
                    n = nb - 1 - j
                    r3 = j % 2
                    act(kb[r3][:], kth[:, n, :], AF.Identity, ['kth', 'wst'], [('kb', r3)], scale=wst[:, 1, hd:hd + 1])

                def B1(j):
                    n = nb - 1 - j
                    r3 = j % 2
                    st_ = 2 * (j % 2)
                    for dc in range(2):
                        mm(ps[st_ + dc][:], kb[r3][:, dc * 128:(dc + 1) * 128], vth[:, n, :], True, True,
                           [('kb', r3), 'vth'], [('ps', st_ + dc)])

                def B2(j):
                    st_ = 2 * (j % 2)
                    src = Sb[j % 2]; dst = Sb[(j + 1) % 2]
                    for dc in range(2):
                        stt('dve', dst[:, dc, :], src[:, dc, :], gC[:, 4 + hd:5 + hd], ps[st_ + dc][:], ALU.mult, ALU.add,
                            [('Sb', j % 2), 'gC', ('ps', st_ + dc)], [('Sb', (j + 1) % 2)])

                def B3(j):
                    n = nb - 1 - j
                    dst = Sb[(j + 1) % 2]
                    if n >= 1:
                        act(Sst[:, n - 1], dst[:], AF.Copy, [('Sb', (j + 1) % 2)], [('Sst', n - 1)])
                    else:
                        dma(D['ret_out'][sq_i, 1, hd], dst[:], [('Sb', (j + 1) % 2)], [('ro', sq_i, 1, hd)])

                pipeline(nbw, [(0, B0), (1, B1), (2, B2), (3, B3)], desc=True)
                act(Sfb[0][:], Sf[0][:], AF.Copy, [('Sf', 0)], [('Sfb', 0)])

                def need_kv(n):
                    return (n < nb - 1) or (not smp)

                def F0(n):
                    cols = slice(n * 128, (n + 1) * 128)
                    if need_kv(n):
                        act(kf2[n % 2][:], kth[:, n, :], AF.Identity, ['kth', 'wst'], [('kf2', n % 2)], scale=wst[:, 0, hd:hd + 1])
                    for dr in range(2):
                        tt('pool', qf[n % NQF][:, dr], qTh[:, :, cols], V(wq, (dr * 4 + hd) * 128, [[0, 2], [1, 128]]), ALU.mult,
                           ['qTh', 'wq'], [('qf', n % NQF)])

                def P1(n):
                    cols = slice(n * 128, (n + 1) * 128)
                    if need_kv(n):
                        st_ = 2 * (n % 2)
                        for dc in range(2):
                            mm(ps[st_ + dc][:], kf2[n % 2][:, dc * 128:(dc + 1) * 128], vth[:, n, :], True, True,
                               [('kf2', n % 2), 'vth'], [('ps', st_ + dc)])
                    mm(ps[4][:, 0:128], kTh[:, 0, cols], qTh[:, 0, cols], True, False, ['kTh', 'qTh'], [('ps', 4)])
                    mm(ps[4][:, 0:128], kTh[:, 1, cols], qTh[:, 1, cols], False, True, ['kTh', 'qTh'], [('ps', 4)])

                def D2(n):
                    r3 = n % 3
                    tt('dve', PTr[r3][:], ps[4][:, 0:128], Dc[:, hd, :], ALU.mult, [('ps', 4), 'Dc'], [('PTr', r3)])
                    if need_kv(n):
                        st_ = 2 * (n % 2)
                        src = Sf[n % 2]; dst = Sf[(n + 1) % 2]
                        for dc in range(2):
                            stt('dve', dst[:, dc, :], src[:, dc, :], gC[:, hd:hd + 1], ps[st_ + dc][:], ALU.mult, ALU.add,
                                [('Sf', n % 2), 'gC', ('ps', st_ + dc)], [('Sf', (n + 1) % 2)])

                def A3(n):
                    if need_kv(n):
                        dst = Sf[(n + 1) % 2]
                        if n < nb - 1:
                            act(Sfb[(n + 1) % 4][:], dst[:], AF.Copy, [('Sf', (n + 1) % 2)], [('Sfb', (n + 1) % 4)])
                        else:
                            dma(D['ret_out'][sq_i, 0, hd], dst[:], [('Sf', (n + 1) % 2)], [('ro', sq_i, 0, hd)])

                def P4(n):
                    r3 = n % 3
                    ob = 5 + n % 2
                    sfb = Sfb[n % 4]
                    q_ = qf[n % NQF]
                    mm(ps[ob][:], PTr[r3][:], vth[:, n, :], True, False, [('PTr', r3), 'vth'], [('ps', ob)])
                    for dc in range(2):
                        mm(ps[ob][:], q_[:, 0, dc, :], sfb[:, dc, :], False, False, [('qf', n % NQF), ('Sfb', n % 4)], [('ps', ob)])
                    for dc in range(2):
                        mm(ps[ob][:], q_[:, 1, dc, :], Sst[:, n, dc, :], False, dc == 1, [('qf', n % NQF), ('Sst', n)], [('ps', ob)])

                def A5(n):
                    ob = 5 + n % 2
                    dma(sgc[n % NSG][:], D['sg_d'][(b0 + n) * 128:(b0 + n + 1) * 128, hd * 512:(hd + 1) * 512], [], [('sgc', n % NSG)])
                    act(Osb[n % NOS][:], ps[ob][:], AF.Copy, [('ps', ob)], [('Osb', n % NOS)])

                def D6(n):
                    u = n % 2
                    S.op('dve', lambda e, u=u, n=n: e.bn_stats(st6[u][:], Osb[n % NOS][:]), [('Osb', n % NOS)], [('st6', u)])
                    S.op('dve', lambda e, u=u, n=n: e.bn_aggr(mv[n % 4][:], st6[u][:]), [('st6', u)], [('mv', n % 4)])

                def A7(n):
                    u = n % 2
                    act(lnv[u][:], mv[n % 4][:, 1:2], AF.Ln, [('mv', n % 4)], [('lnv', u)], bias=epsb[:, 0:1], scale=1.0)
                    act(rsd[n % 3][:], lnv[u][:], AF.Exp, [('lnv', u)], [('rsd', n % 3)], scale=-0.5)

                def D8(n):
                    o_ = Osb[n % NOS]
                    ts('dve', o_[:], o_[:], mv[n % 4][:, 0:1], rsd[n % 3][:], ALU.subtract, ALU.mult,
                       [('Osb', n % NOS), ('mv', n % 4), ('rsd', n % 3)], [('Osb', n % NOS)])

                def Q9(n):
                    tt('pool', uu[n % 2][:], Osb[n % NOS][:], sgc[n % NSG][:], ALU.mult,
                       [('Osb', n % NOS), ('sgc', n % NSG)], [('uu', n % 2)])

                def P10(n):
                    for e_ in range(4):
                        S.op('pe', lambda e, e_=e_, n=n: e.transpose(psbf[:, e_ * 128:(e_ + 1) * 128],
                                                                     uu[n % 2][:, e_ * 128:(e_ + 1) * 128], ident_bf[:]),
                             [('uu', n % 2), 'ident'], [('ps', 7)])

                def A11(n):
                    u = n % 2
                    act(uT[u][:], psbf3(4), AF.Copy, [('ps', 7)], [('uT', u)])
                    dma(rows(D['uT_d'])[:, hd * 4:(hd + 1) * 4, (b0 + n) * 128:(b0 + n + 1) * 128], uT[u][:],
                        [('uT', u)], [('uTd', b0 + n, hd)])

                pipeline(nb, [(0, F0), (1, P1), (2, D2), (3, A3), (4, P4), (5, A5), (6, D6), (7, A7), (8, D8),
                              (9, Q9), (10, P10), (11, A11)], desc=True)
        S.sb_reset(mark)
        phase_outproj(l, 16, D['ret_wout'], D['uT_d'], x_src, x_dst)

    def phase_lru(l, x_src, x_dst):
        S.phase = 'L1'
        mark = S.sb_mark()
        T = 512
        nt = NTOK // T
        w = S.sb([128, 8, 2048], BF16, "lwin")
        for k in range(8):
            cast_load(w[:, k, :], rows(D['lru_win'])[:, k, :], 'lwin')
        nm = NM(T)
        gt = [S.sb([128, 8, T], BF16, f"gt{i}") for i in range(2)]
        xr = [S.sb([128, 8, T], F32, f"xrt{i}") for i in range(2)]
        nm.load(0, x_src); nm.load(1, x_src)
        cnt = 0
        hs = {0: nm.run(0, l, 0, psb=6)}
        for i in range(nt):
            par = i % 2
            h, hk, sidx = hs.pop(i)
            for c in range(8):
                if i + 1 < nt:
                    if c == 1:
                        pend, hs[i + 1] = nm.parts(i + 1, l, 0, psb=6)
                    if 1 <= c <= 5:
                        pend[c - 1]()
                b = cnt % 4
                cnt += 1
                for k in range(8):
                    mm(ps[b][:], w[:, k, c * 128:(c + 1) * 128], h[:, k, :], k == 0, k == 7, [hk, 'lwin'], [('ps', b)])
                act(gt[par][:, c, :], ps[b][:], AF.Gelu_apprx_tanh, [('ps', b)], [('gt', par)])
                b = cnt % 4
                cnt += 1
                for k in range(8):
                    mm(ps[b][:], w[:, k, 1024 + c * 128:1024 + (c + 1) * 128], h[:, k, :], k == 0, k == 7, [hk, 'lwin'], [('ps', b)])
                S.op('dve', lambda e, par=par, c=c, b=b: e.tensor_copy(xr[par][:, c, :], ps[b][:]), [('ps', b)], [('xrt', par)])
            dma(rows(D['gate_d'])[:, :, i * T:(i + 1) * T], gt[par][:], [('gt', par)], [('gd', i)])
            dma(rows(D['xr_d'])[:, :, i * T:(i + 1) * T], xr[par][:], [('xrt', par)], [('xrd', i)])
            if i + 2 < nt:
                nm.load(i + 2, x_src)
        S.sb_reset(mark)
        S.phase = 'L2'
        cv = S.sb([128, 5, 8], F32, "cv"); dma(cv[:], D['convT'], [], ['cv'])
        wr = S.sb([128, 2, 8, 128], BF16, "wr"); cast_load(wr[:], D['lru_wr'], 'wr')
        wi = S.sb([128, 2, 8, 128], BF16, "wi"); cast_load(wi[:], D['lru_wi'], 'wi')
        bT = S.sb([128, 3, 2, 8], F32, "lbT"); dma(bT[:], D['lru_bT'], [], ['lbT'])
        h0 = S.sb([128, 2, 8], F32, "h0"); dma(h0[:], D['slru'], [], ['h0'])
        clam = S.sb([128, 2, 8], F32, "clam"); ctmp = S.sb([128, 2, 8], F32, "ctmp")
        act(ctmp[:], bT[:, 2], AF.Exp, ['lbT'], ['ctmp'], scale=-1.0)
        act(ctmp[:], ctmp[:], AF.Ln, ['ctmp'], ['ctmp'], bias=oneb[:, 0:1])
        ts1('dve', clam[:], ctmp[:], -8.0, ALU.mult, ['ctmp'], ['clam'])
        XR = [S.sb([128, NTOK], F32, f"XR{i}") for i in range(2)]
        G = [S.sb([128, NTOK], BF16, f"G{i}") for i in range(2)]
        XC = S.sb([128, NTOK], F32, "XC")
        RAd = [S.sb([128, NTOK], F32, f"RA{i}") for i in range(2)]
        IUd = [S.sb([128, NTOK], F32, f"IU{i}") for i in range(2)]
        HF = S.sb([128, NTOK], F32, "HF"); HB = S.sb([128, NTOK], F32, "HB")
        XCB = S.sb([128, NTOK], BF16, "XCB")
        stc = S.sb([128, 2, 2], F32, "stc")
        segs = [(0, NS, True), (NS, 256, False), (NS + 256, 256, False)]
        xck = [('XC', 0), ('XC', 1), ('XC', 2)]
        hfk = [('HF', 0), ('HF', 1), ('HF', 2)]
        hbk = [('HB', 0), ('HB', 1), ('HB', 2)]

        def loadc(c):
            p = c % 2
            dma(XR[p][:], D['xr_d'][c * 128:(c + 1) * 128, :], [], [('XR', p)])
            dma(G[p][:], D['gate_d'][c * 128:(c + 1) * 128, :], [], [('G', p)])
        loadc(0)
        cnt = 0
        for c in range(8):
            p = c % 2
            if c + 1 < 8:
                loadc(c + 1)
            act(XC[:], XR[p][:], AF.Identity, [('XR', p), 'cv'], xck, scale=cv[:, 2, c:c + 1], bias=cv[:, 4, c:c + 1])
            for si, (s0, L, smp) in enumerate(segs):
                eng = 'dve'
                for (tap, do, di, n_) in ((0, 2, 0, L - 2), (1, 1, 0, L - 1), (3, 0, 1, L - 1)):
                    stt(eng, XC[:, s0 + do:s0 + do + n_], XR[p][:, s0 + di:s0 + di + n_], cv[:, tap, c:c + 1],
                        XC[:, s0 + do:s0 + do + n_], ALU.mult, ALU.add, [('XR', p), 'cv', ('XC', si)], [('XC', si)])
            act(XCB[:], XC[:], AF.Copy, xck, ['XCB'])
            for dr in range(2):
                RA = RAd[dr]; IU = IUd[dr]
                TMP = HF if dr == 0 else HB
                tk = hfk if dr == 0 else hbk
                for ti in range(nt):
                    cols = slice(ti * T, (ti + 1) * T)
                    b = cnt % 4
                    cnt += 1
                    mm(ps[b][:], wr[:, dr, c, :], XCB[:, cols], True, True, ['XCB', 'wr'], [('ps', b)])
                    act(RA[:, cols], ps[b][:], AF.Sigmoid, [('ps', b), 'lbT'], [('RA', dr)], bias=bT[:, 0, dr, c:c + 1])
                act(RA[:], RA[:], AF.Exp, [('RA', dr), 'clam'], [('RA', dr)], scale=clam[:, dr, c:c + 1])
                for ti in range(nt):
                    cols = slice(ti * T, (ti + 1) * T)
                    b = cnt % 4
                    cnt += 1
                    mm(ps[b][:], wi[:, dr, c, :], XCB[:, cols], True, True, ['XCB', 'wi'], [('ps', b)])
                    act(IU[:, cols], ps[b][:], AF.Sigmoid, [('ps', b), 'lbT'], [('IU', dr)], bias=bT[:, 1, dr, c:c + 1])
                tt('pool', IU[:], IU[:], XC[:], ALU.mult, [('IU', dr)] + xck, [('IU', dr)])
                act(TMP[:], RA[:], AF.Square, [('RA', dr)], tk)
                act(TMP[:], TMP[:], AF.Sqrt, tk, tk, scale=-1.0, bias=oneb[:, 0:1])
            for dr in range(2):
                RA = RAd[dr]; IU = IUd[dr]
                TMP = HF if dr == 0 else HB
                tk = hfk if dr == 0 else hbk
                tt('dve', IU[:], IU[:], TMP[:], ALU.mult, [('IU', dr)] + tk, [('IU', dr)])
                for si, (s0, L, smp) in enumerate(segs):
                    init = h0[:, dr, c:c + 1] if smp else 0.0
                    if dr == 0:
                        S.op('dve', lambda e, s0=s0, L=L, init=init, RA=RA, IU=IU: e.tensor_tensor_scan(
                            HF[:, s0:s0 + L], RA[:, s0:s0 + L], IU[:, s0:s0 + L], init, ALU.mult, ALU.add),
                            [('RA', dr), ('IU', dr), 'h0'], [('HF', si)])
                    else:
                        S.op('dve', lambda e, s0=s0, L=L, init=init, RA=RA, IU=IU: e.tensor_tensor_scan(
                            V(HB, s0 + L - 1, [[-1, L]]), V(RA, s0 + L - 1, [[-1, L]]), V(IU, s0 + L - 1, [[-1, L]]),
                            init, ALU.mult, ALU.add), [('RA', dr), ('IU', dr), 'h0'], [('HB', si)])
            for si in (1, 2):
                s0, L, _ = segs[si]
                S.op('pool', lambda e, si=si, s0=s0, L=L: e.tensor_copy(stc[:, si - 1, 0:1], HF[:, s0 + L - 1:s0 + L]),
                     [('HF', si)], ['stc'])
                S.op('pool', lambda e, si=si, s0=s0: e.tensor_copy(stc[:, si - 1, 1:2], HB[:, s0:s0 + 1]),
                     [('HB', si)], ['stc'])
                for dr in range(2):
                    dma(D['lru_out'][si - 1, dr, c * 128:(c + 1) * 128].rearrange("(p o) -> p o", o=1),
                        stc[:, si - 1, dr:dr + 1], ['stc'], [('lo', si, dr, c)])
            hk_ = [('HF', 0), ('HF', 1), ('HF', 2), ('HB', 0), ('HB', 1), ('HB', 2)]
            tt('pool', HB[:], HF[:], HB[:], ALU.add, hk_, [('HB', 0), ('HB', 1), ('HB', 2)])
            tt('pool', G[p][:], G[p][:], HB[:], ALU.mult, [('G', p), ('HB', 0), ('HB', 1), ('HB', 2)], [('G', p)])
            dma(D['yin_d'][c * 128:(c + 1) * 128, :], G[p][:], [('G', p)], [('yd', c)])
        S.sb_reset(mark)
        phase_outproj(l, 8, D['lru_wout'], D['yin_d'], x_src, x_dst)

    phase_mod()
    S.sb_reset(base_mark)
    cur = D['xT']
    for l in range(depth_run):
        kind = l % 3
        last = l == depth_run - 1
        if kind == 0:
            phase_attn(l, l // 3, cur, D['xres'])
        elif kind == 1:
            phase_ret(l, cur, D['xres'])
        else:
            phase_lru(l, cur, D['xres'])
        cur = D['xres']
        phase_mlp(l, cur, D['yT'] if last else D['xres'])
    S.emit()
    return nc


def _f32(a):
    return np.ascontiguousarray(np.asarray(a, dtype=np.float32))


RET_PERM = np.concatenate([np.arange(0, 64), np.arange(128, 192), np.arange(64, 128), np.arange(192, 256)])


def _rope_tables_attn():
    p = np.arange(128)
    d = p % 64
    t = np.arange(NS)
    row = (t // 64).astype(np.float32)
    col = (t % 64).astype(np.float32)
    j = np.where(d < 32, d % 16, (d - 32) % 16).astype(np.float32)
    inv = (10000.0 ** (-j / 16.0)).astype(np.float32)
    pos = np.where((d < 32)[:, None], row[None, :], col[None, :]).astype(np.float32)
    ang = (pos * inv[:, None]).astype(np.float32)
    return _f32(np.cos(ang)), _f32(np.sin(ang))


def _rope_tables_ret():
    p = np.arange(128)
    t = np.arange(NS)
    row = (t // 64).astype(np.float32)
    col = (t % 64).astype(np.float32)
    j = (p % 64).astype(np.float32)
    inv = (10000.0 ** (-j / 64.0)).astype(np.float32)
    pos = np.where((p < 64)[:, None], row[None, :], col[None, :]).astype(np.float32)
    ang = (pos * inv[:, None]).astype(np.float32)
    return _f32(np.cos(ang)), _f32(np.sin(ang))


def prep_shared(inp):
    sh = {}
    sh['w_ada'] = _f32(inp['w_ada'])
    sh['b_adaT'] = _f32(np.asarray(inp['b_ada']).reshape(4, 48, 128).transpose(2, 0, 1))
    nm = np.stack([np.asarray(inp['norm_mix']), np.asarray(inp['norm_mlp'])])
    sh['nmT'] = _f32(nm.reshape(2, 4, 8, 128).transpose(3, 0, 1, 2))
    sh['w_up'] = _f32(inp['w_up']); sh['w_down'] = _f32(inp['w_down'])
    W = np.asarray(inp['attn_w_in'])
    q = W[:, :, :1024]; k = W[:, :, 1024:1280]; v = W[:, :, 1280:1536]
    kd = np.concatenate([np.concatenate([k[:, :, g * 64:(g + 1) * 64]] * 2, axis=2) for g in range(4)], axis=2)
    sh['attn_win'] = _f32(np.concatenate([q, kd, v], axis=2))
    sh['attn_wout'] = _f32(inp['attn_w_out'])
    pidx = np.arange(128) % 64
    sh['qgT'] = _f32(np.asarray(inp['attn_q_gain'])[:, pidx].T)
    sh['kgT'] = _f32(np.asarray(inp['attn_k_gain'])[:, pidx].T)
    sh['sinkB'] = _f32(np.broadcast_to(np.asarray(inp['attn_sink'])[None], (128, 2, 16)))
    sh['cosA'], sh['sinA'] = _rope_tables_attn()
    R = np.zeros((128, 128), np.float32)
    for m in range(128):
        if (m % 32) < 16:
            R[m + 16, m] = -1.0
        else:
            R[m - 16, m] = 1.0
    sh['rotA'] = R
    kk = np.arange(128)
    sh['bdones'] = _f32((kk[:, None] // 64) == (kk[None, :] // 64))
    sh['ident'] = _f32(np.eye(128))
    sh['mlo'] = _f32(kk[:, None] >= kk[None, :])
    sh['mhi'] = _f32(kk[:, None] <= kk[None, :])
    sh['mgt'] = _f32(kk[:, None] > kk[None, :])
    Wr = np.asarray(inp['ret_w_in'])[0]
    qc = np.concatenate([Wr[:, hd * 256 + RET_PERM] for hd in range(4)], axis=1)
    kc = np.concatenate([Wr[:, 1024 + hd * 256 + RET_PERM] for hd in range(4)], axis=1)
    sh['ret_win'] = _f32(np.concatenate([qc, kc, Wr[:, 2048:]], axis=1))
    sh['ret_wout'] = _f32(np.asarray(inp['ret_w_out'])[0])
    sh['gainB'] = _f32(np.broadcast_to(np.asarray(inp['ret_gn_gain'])[0][None], (128, 2048)))
    sh['ldB'] = _f32(np.broadcast_to(np.asarray(inp['ret_log_decay'])[0].reshape(8)[None], (128, 8)))
    sh['cosR'], sh['sinR'] = _rope_tables_ret()
    jj = kk[:, None].astype(np.float32); ii = kk[None, :].astype(np.float32)
    sh['MfT'] = _f32(np.maximum(ii - jj, 0)); sh['MbT'] = _f32(np.maximum(jj - ii, 0))
    sh['iq1'] = _f32(np.broadcast_to(ii + 1.0, (128, 128))); sh['iqb'] = _f32(np.broadcast_to(128.0 - ii, (128, 128)))
    sh['cstf'] = _f32(127.0 - jj); sh['cstb'] = _f32(jj)
    sh['lru_win'] = _f32(np.asarray(inp['lru_w_in'])[0]); sh['lru_wout'] = _f32(np.asarray(inp['lru_w_out'])[0])
    cw = np.asarray(inp['lru_conv_w'])[0]; cb = np.asarray(inp['lru_conv_b'])[0]
    cv = np.concatenate([cw, cb[None]], axis=0)
    sh['convT'] = _f32(cv.reshape(5, 8, 128).transpose(2, 0, 1))
    sh['lru_wr'] = _f32(np.asarray(inp['lru_w_r'])[0].transpose(2, 0, 1, 3))
    sh['lru_wi'] = _f32(np.asarray(inp['lru_w_i'])[0].transpose(2, 0, 1, 3))
    b3 = np.stack([np.asarray(inp['lru_b_r'])[0], np.asarray(inp['lru_b_i'])[0], np.asarray(inp['lru_lambda'])[0]])
    sh['lru_bT'] = _f32(b3.reshape(3, 2, 8, 128).transpose(3, 0, 1, 2))
    return sh


def prep_core(inp, b):
    m = {}
    xs = np.asarray(inp['x_sample'])[b]
    xp = np.asarray(inp['x_prompt'])[2 * b:2 * b + 2].reshape(512, 1024)
    m['xT'] = _f32(np.concatenate([xs, xp], axis=0).T)
    cc = np.stack([np.asarray(inp['c'])[b], np.asarray(inp['c_ctx'])], axis=1)
    m['cT'] = _f32(cc.reshape(8, 128, 2).transpose(1, 0, 2))
    ck = np.asarray(inp['cache_attn_k'])[b]
    kt = ck.transpose(0, 3, 2, 1)
    m['kctx'] = _f32(np.concatenate([kt, kt], axis=1))
    cvv = np.asarray(inp['cache_attn_v'])[b]
    m['vctx'] = _f32(cvv.reshape(2, 4, 128, 4, 64).transpose(0, 2, 1, 3, 4))
    sr = np.asarray(inp['state_ret'])[b, 0]
    sr = sr[:, :, RET_PERM, :].reshape(2, 4, 2, 128, 512).transpose(0, 1, 3, 2, 4)
    m['sret'] = _f32(sr)
    sl = np.asarray(inp['state_lru'])[b, 0]
    m['slru'] = _f32(sl.reshape(2, 8, 128).transpose(2, 0, 1))
    return m


_NC_CACHE = {}


def kernel(**inputs):
    depth_run = int(os.environ.get("MK_DEPTH", "4"))
    ncores = int(os.environ.get("MK_CORES", "8"))
    if depth_run not in _NC_CACHE:
        _NC_CACHE[depth_run] = build_nc(depth_run)
    nc = _NC_CACHE[depth_run]
    sh = prep_shared(inputs)
    in_maps = []
    for b in range(ncores):
        m = dict(sh)
        m.update(prep_core(inputs, b))
        in_maps.append(m)
    res = run_bass_kernel_spmd(nc, in_maps, core_ids=list(range(ncores)))
    nb = ncores
    y_prompt = np.zeros((2 * nb, 256, 1024), np.float32)
    y_sample = np.zeros((nb, 4096, 1024), np.float32)
    nk = np.zeros((2 * nb, 2, 256, 4, 64), np.float32)
    nv = np.zeros((2 * nb, 2, 256, 4, 64), np.float32)
    nret = np.zeros((2 * nb, 1, 2, 4, 256, 512), np.float32)
    nlru = np.zeros((2 * nb, 1, 2, 1024), np.float32)
    inv = np.argsort(RET_PERM)
    for b in range(nb):
        r = res.results[b]
        yT = r['yT']
        y_sample[b] = yT[:, :NS].T
        for s in range(2):
            y_prompt[2 * b + s] = yT[:, NS + 256 * s:NS + 256 * (s + 1)].T
        ko = r['k_out'].reshape(2, 4, 64, 2, 256)
        nk[2 * b:2 * b + 2] = ko.transpose(3, 0, 4, 1, 2)
        vo = r['v_out'].reshape(2, 2, 256, 4, 64)
        nv[2 * b:2 * b + 2] = vo.transpose(1, 0, 2, 3, 4)
        ro = r['ret_out'].transpose(0, 1, 2, 4, 3, 5).reshape(2, 2, 4, 256, 512)
        nret[2 * b:2 * b + 2, 0] = ro[:, :, :, inv, :]
        nlru[2 * b:2 * b + 2, 0] = r['lru_out']
    return (y_prompt, y_sample, nk, nv, nret, nlru)
```
